# Optimizing a Trainium2 kernel written in Bass

```python
import math
import jax
import jax.numpy as jnp
from jax import lax
import numpy as np

D_MODEL = 4096
BATCH = 8
SEQ = 2048
DEPTH = 4

CHUNK = 64
Q_BLOCK = 128
N_MIXERS = 4
FFN_DIM = 5632
D_COND = 256
EPS = 1e-6

SC_WIDTH = 3
MLA_HEADS = 32
MLA_Q_RANK = 1024
MLA_KV_RANK = 512
MLA_NOPE = 128
MLA_ROPE = 64
MLA_V = 128
MLA_QK = MLA_NOPE + MLA_ROPE
ROPE_BASE = 10000.0
LRU_WIDTH = D_MODEL
LRU_BLOCKS = 16
LRU_BLOCK = LRU_WIDTH // LRU_BLOCKS
LRU_CONV = 4
LRU_C = 8.0
GDN_K_HEADS = 16
GDN_V_HEADS = 32
GDN_HEAD = 128
GDN_CONV = 4
GDN_KD = GDN_K_HEADS * GDN_HEAD
GDN_VD = GDN_V_HEADS * GDN_HEAD
GDN_QKV = 2 * GDN_KD + GDN_VD
GDN_IN = GDN_QKV + GDN_VD + 2 * GDN_V_HEADS

kernel_name = 'hybrid_streaming_encoder_trunk'


def _n_uses(m):
    return len(range(m, DEPTH, N_MIXERS))


def rmsnorm(x, g):
    xf = x.astype(jnp.float32)
    y = xf * lax.rsqrt(jnp.mean(xf * xf, axis=-1, keepdims=True) + EPS)
    return (y * g.astype(jnp.float32)).astype(x.dtype)


def l2norm(t):
    return t * lax.rsqrt(jnp.sum(t * t, axis=-1, keepdims=True) + EPS)


def adaln(x, g, shift, scale):
    return rmsnorm(x, g) * (1 + scale[:, None, :]) + shift[:, None, :]


def causal_dwconv(x, w):
    k, ch = w.shape
    return lax.conv_general_dilated(x, w.astype(x.dtype)[:, None, :], window_strides=(1,),
                                    padding=[(k - 1, 0)], dimension_numbers=('NWC', 'WIO', 'NWC'),
                                    feature_group_count=ch)


def swiglu(h, w_in, w_out):
    gt, up = jnp.split(h @ w_in, 2, axis=-1)
    return (jax.nn.silu(gt) * up) @ w_out


def rope_tables(positions):
    inv = ROPE_BASE ** (-jnp.arange(0, MLA_ROPE, 2, dtype=jnp.float32) / MLA_ROPE)
    ang = positions.astype(jnp.float32)[..., None] * inv
    return jnp.cos(ang)[:, :, None, :], jnp.sin(ang)[:, :, None, :]


def apply_rope(t, cos, sin):
    t1, t2 = jnp.split(t.astype(jnp.float32), 2, axis=-1)
    return jnp.concatenate([t1 * cos - t2 * sin, t2 * cos + t1 * sin], axis=-1).astype(t.dtype)


def short_conv_mixer(h, w_in, conv_w, w_out):
    b_gate, c_gate, xin = jnp.split(h @ w_in, 3, axis=-1)
    return (b_gate * causal_dwconv(c_gate * xin, conv_w)) @ w_out


def mla_mixer(h, cos, sin, w_in, q_lat_g, kv_lat_g, w_uq, w_ukv, q_norm_g, k_norm_g, w_o):
    B, S, _ = h.shape
    lat = h @ w_in
    q_lat = lat[..., :MLA_Q_RANK]
    kv_lat = lat[..., MLA_Q_RANK:MLA_Q_RANK + MLA_KV_RANK]
    k_pe = lat[..., MLA_Q_RANK + MLA_KV_RANK:]
    q = (rmsnorm(q_lat, q_lat_g) @ w_uq).reshape(B, S, MLA_HEADS, MLA_QK)
    kv = (rmsnorm(kv_lat, kv_lat_g) @ w_ukv).reshape(B, S, MLA_HEADS, MLA_NOPE + MLA_V)
    k_nope, v = kv[..., :MLA_NOPE], kv[..., MLA_NOPE:]
    k = jnp.concatenate([k_nope, jnp.broadcast_to(k_pe[:, :, None, :], (B, S, MLA_HEADS, MLA_ROPE))], axis=-1)
    q = rmsnorm(q, q_norm_g)
    k = rmsnorm(k, k_norm_g)
    q = jnp.concatenate([q[..., :MLA_NOPE], apply_rope(q[..., MLA_NOPE:], cos, sin)], axis=-1)
    k = jnp.concatenate([k[..., :MLA_NOPE], apply_rope(k[..., MLA_NOPE:], cos, sin)], axis=-1)
    n_blk = S // Q_BLOCK
    q_blocks = q.reshape(B, n_blk, Q_BLOCK, MLA_HEADS, MLA_QK).swapaxes(0, 1)
    key_chunk = jnp.arange(S) // CHUNK
    scale = MLA_QK ** -0.5

    def attend(args):
        qb, blk = args
        q_chunk = (blk * Q_BLOCK + jnp.arange(Q_BLOCK)) // CHUNK
        mask = key_chunk[None, :] <= q_chunk[:, None]
        s = jnp.einsum('bqhd,bkhd->bhqk', qb, k, preferred_element_type=jnp.float32) * scale
        p = jax.nn.softmax(jnp.where(mask, s, -1e30), axis=-1)
        return jnp.einsum('bhqk,bkhd->bqhd', p.astype(v.dtype), v)

    o = lax.map(attend, (q_blocks, jnp.arange(n_blk)))
    o = o.swapaxes(0, 1).reshape(B, S, MLA_HEADS * MLA_V)
    return o @ w_o


def rglru_mixer(h, w_in, conv_w, conv_b, w_a, b_a, w_x, b_x, lam, w_out):
    B, S, _ = h.shape
    gate, xr = jnp.split(h @ w_in, 2, axis=-1)
    xr = causal_dwconv(xr, conv_w) + conv_b
    xb = xr.reshape(B, S, LRU_BLOCKS, LRU_BLOCK)
    r = jax.nn.sigmoid((jnp.einsum('bsnd,nde->bsne', xb, w_a).reshape(B, S, LRU_WIDTH) + b_a).astype(jnp.float32))
    i = jax.nn.sigmoid((jnp.einsum('bsnd,nde->bsne', xb, w_x).reshape(B, S, LRU_WIDTH) + b_x).astype(jnp.float32))
    log_a = -LRU_C * r * jax.nn.softplus(-lam.astype(jnp.float32))
    a = jnp.exp(log_a)
    u = jnp.sqrt(-jnp.expm1(2.0 * log_a)) * (i * xr.astype(jnp.float32))

    def combine(left, right):
        a_l, b_l = left
        a_r, b_r = right
        return a_l * a_r, a_r * b_l + b_r

    _, hs = lax.associative_scan(combine, (a, u), axis=1)
    return (hs.astype(h.dtype) * jax.nn.gelu(gate)) @ w_out


def chunk_gated_delta_rule(q, k, v, g, beta):
    B, S, H, dk = q.shape
    dv = v.shape[-1]
    n = S // CHUNK

    def to_chunks(t):
        return t.reshape(B, n, CHUNK, H, -1).transpose(1, 0, 3, 2, 4)

    q, k, v = to_chunks(q), to_chunks(k), to_chunks(v)
    g = jnp.cumsum(g.reshape(B, n, CHUNK, H).transpose(1, 0, 3, 2), axis=-1)
    beta = beta.reshape(B, n, CHUNK, H).transpose(1, 0, 3, 2)
    tri = jnp.tril(jnp.ones((CHUNK, CHUNK), dtype=bool))
    strict = jnp.tril(jnp.ones((CHUNK, CHUNK), dtype=bool), k=-1)
    diff = g[..., :, None] - g[..., None, :]
    decay = jnp.where(tri, jnp.exp(jnp.where(tri, diff, 0.0)), 0.0)
    kb = k * beta[..., None]
    m = jnp.where(strict, jnp.einsum('nbhik,nbhjk->nbhij', kb, k) * decay, 0.0)
    eye = jnp.eye(CHUNK, dtype=jnp.float32)
    t_inv = lax.linalg.triangular_solve(m + eye, jnp.broadcast_to(eye, m.shape), left_side=True,
                                        lower=True, unit_diagonal=True)
    u = jnp.einsum('nbhij,nbhjv->nbhiv', t_inv, v * beta[..., None])
    w = jnp.einsum('nbhij,nbhjk->nbhik', t_inv, kb * jnp.exp(g)[..., None])
    a_intra = jnp.einsum('nbhik,nbhjk->nbhij', q, k) * decay

    def step(state, xs):
        q_i, k_i, u_i, w_i, g_i, a_i = xs
        v_new = u_i - jnp.einsum('bhck,bhkv->bhcv', w_i, state)
        o = (jnp.einsum('bhck,bhkv->bhcv', q_i * jnp.exp(g_i)[..., None], state)
             + jnp.einsum('bhcj,bhjv->bhcv', a_i, v_new))
        g_last = g_i[..., -1]
        state = (state * jnp.exp(g_last)[..., None, None]
                 + jnp.einsum('bhck,bhcv->bhkv', k_i * jnp.exp(g_last[..., None] - g_i)[..., None], v_new))
        return state, o

    state0 = jnp.zeros((B, H, dk, dv), jnp.float32)
    _, o = lax.scan(step, state0, (q, k, u, w, g, a_intra))
    return o.transpose(1, 0, 3, 2, 4).reshape(B, S, H, dv)


def gdn_mixer(h, w_in, conv_w, a_log, dt_bias, o_norm_g, w_out):
    B, S, _ = h.shape
    f32 = jnp.float32
    proj = h @ w_in
    qkv = jax.nn.silu(causal_dwconv(proj[..., :GDN_QKV], conv_w))
    z = proj[..., GDN_QKV:GDN_QKV + GDN_VD].reshape(B, S, GDN_V_HEADS, GDN_HEAD)
    b_raw = proj[..., GDN_QKV + GDN_VD:GDN_QKV + GDN_VD + GDN_V_HEADS]
    a_raw = proj[..., GDN_QKV + GDN_VD + GDN_V_HEADS:]
    q = qkv[..., :GDN_KD].reshape(B, S, GDN_K_HEADS, GDN_HEAD).astype(f32)
    k = qkv[..., GDN_KD:2 * GDN_KD].reshape(B, S, GDN_K_HEADS, GDN_HEAD).astype(f32)
    v = qkv[..., 2 * GDN_KD:].reshape(B, S, GDN_V_HEADS, GDN_HEAD).astype(f32)
    rep = GDN_V_HEADS // GDN_K_HEADS
    q = jnp.repeat(l2norm(q), rep, axis=2) * (GDN_HEAD ** -0.5)
    k = jnp.repeat(l2norm(k), rep, axis=2)
    beta = jax.nn.sigmoid(b_raw.astype(f32))
    g = -jnp.exp(a_log.astype(f32)) * jax.nn.softplus(a_raw.astype(f32) + dt_bias.astype(f32))
    o = chunk_gated_delta_rule(q, k, v, g, beta)
    o = rmsnorm(o, o_norm_g) * jax.nn.silu(z.astype(f32))
    return o.reshape(B, S, GDN_VD).astype(h.dtype) @ w_out


def setup_inputs(seed: int = 0):
    key = jax.random.key(seed)
    keys = iter(jax.random.split(key, 64))
    f32 = jnp.float32

    def nrm(shape, fan_in, mult=1.0):
        return jax.random.normal(next(keys), shape, f32) * (mult * fan_in ** -0.5)

    def gain(shape):
        return 1.0 + 0.1 * jax.random.normal(next(keys), shape, f32)

    def bias(shape):
        return 0.01 * jax.random.normal(next(keys), shape, f32)

    n_a, n_b, n_c, n_d = (_n_uses(m) for m in range(N_MIXERS))
    x = jax.random.normal(next(keys), (BATCH, SEQ, D_MODEL), f32)
    c = jax.random.normal(next(keys), (BATCH, D_MODEL), f32)
    offsets = jax.random.randint(next(keys), (BATCH, 1), 0, 8192, dtype=jnp.int32)
    positions = offsets + jnp.arange(SEQ, dtype=jnp.int32)[None, :]
    a0 = jax.random.uniform(next(keys), (n_c, LRU_WIDTH), f32, 0.9, 0.999)
    a_init = jax.random.uniform(next(keys), (n_d, GDN_V_HEADS), f32, 1.0, 16.0)
    dt = jnp.exp(jax.random.uniform(next(keys), (n_d, GDN_V_HEADS), f32, math.log(1e-3), math.log(1e-1)))
    return {
        'x': x,
        'c': c,
        'positions': positions,
        'cond_w': nrm((D_MODEL, D_COND), D_MODEL),
        'cond_b': bias((D_COND,)),
        'mod_w': nrm((DEPTH, D_COND, 9 * D_MODEL), D_COND, 0.5),
        'mod_b': bias((DEPTH, 9 * D_MODEL)),
        'norm_g': gain((DEPTH, 3, D_MODEL)),
        'ffn_w_in': nrm((DEPTH, 2, D_MODEL, 2 * FFN_DIM), D_MODEL),
        'ffn_w_out': nrm((DEPTH, 2, FFN_DIM, D_MODEL), FFN_DIM),
        'sc_w_in': nrm((n_a, D_MODEL, 3 * D_MODEL), D_MODEL),
        'sc_conv_w': nrm((n_a, SC_WIDTH, D_MODEL), SC_WIDTH),
        'sc_w_out': nrm((n_a, D_MODEL, D_MODEL), D_MODEL),
        'mla_w_in': nrm((n_b, D_MODEL, MLA_Q_RANK + MLA_KV_RANK + MLA_ROPE), D_MODEL),
        'mla_q_lat_g': gain((n_b, MLA_Q_RANK)),
        'mla_kv_lat_g': gain((n_b, MLA_KV_RANK)),
        'mla_w_uq': nrm((n_b, MLA_Q_RANK, MLA_HEADS * MLA_QK), MLA_Q_RANK),
        'mla_w_ukv': nrm((n_b, MLA_KV_RANK, MLA_HEADS * (MLA_NOPE + MLA_V)), MLA_KV_RANK),
        'mla_q_norm_g': gain((n_b, MLA_QK)),
        'mla_k_norm_g': gain((n_b, MLA_QK)),
        'mla_w_o': nrm((n_b, MLA_HEADS * MLA_V, D_MODEL), MLA_HEADS * MLA_V),
        'lru_w_in': nrm((n_c, D_MODEL, 2 * LRU_WIDTH), D_MODEL),
        'lru_conv_w': nrm((n_c, LRU_CONV, LRU_WIDTH), LRU_CONV),
        'lru_conv_b': bias((n_c, LRU_WIDTH)),
        'lru_w_a': nrm((n_c, LRU_BLOCKS, LRU_BLOCK, LRU_BLOCK), LRU_BLOCK),
        'lru_b_a': bias((n_c, LRU_WIDTH)),
        'lru_w_x': nrm((n_c, LRU_BLOCKS, LRU_BLOCK, LRU_BLOCK), LRU_BLOCK),
        'lru_b_x': bias((n_c, LRU_WIDTH)),
        'lru_lam': jnp.log(a0) - jnp.log1p(-a0),
        'lru_w_out': nrm((n_c, LRU_WIDTH, D_MODEL), LRU_WIDTH),
        'gdn_w_in': nrm((n_d, D_MODEL, GDN_IN), D_MODEL),
        'gdn_conv_w': nrm((n_d, GDN_CONV, GDN_QKV), GDN_CONV),
        'gdn_a_log': jnp.log(a_init),
        'gdn_dt_bias': dt + jnp.log(-jnp.expm1(-dt)),
        'gdn_o_norm_g': gain((n_d, GDN_HEAD)),
        'gdn_w_out': nrm((n_d, GDN_VD, D_MODEL), GDN_VD),
    }


def reference(x, c, positions, cond_w, cond_b, mod_w, mod_b, norm_g, ffn_w_in, ffn_w_out,
              sc_w_in, sc_conv_w, sc_w_out,
              mla_w_in, mla_q_lat_g, mla_kv_lat_g, mla_w_uq, mla_w_ukv, mla_q_norm_g, mla_k_norm_g, mla_w_o,
              lru_w_in, lru_conv_w, lru_conv_b, lru_w_a, lru_b_a, lru_w_x, lru_b_x, lru_lam, lru_w_out,
              gdn_w_in, gdn_conv_w, gdn_a_log, gdn_dt_bias, gdn_o_norm_g, gdn_w_out):
    B, S, D = x.shape
    cos, sin = rope_tables(positions)
    cond = jax.nn.silu(c @ cond_w + cond_b)
    for i in range(DEPTH):
        m, j = i % N_MIXERS, i // N_MIXERS
        mod = (cond @ mod_w[i] + mod_b[i]).reshape(B, 3, 3, D)
        shift, scale, gate = mod[:, :, 0], mod[:, :, 1], mod[:, :, 2]
        h = adaln(x, norm_g[i, 0], shift[:, 0], scale[:, 0])
        x = x + 0.5 * (1 + gate[:, 0, None, :]) * swiglu(h, ffn_w_in[i, 0], ffn_w_out[i, 0])
        h = adaln(x, norm_g[i, 1], shift[:, 1], scale[:, 1])
        if m == 0:
            mix = short_conv_mixer(h, sc_w_in[j], sc_conv_w[j], sc_w_out[j])
        elif m == 1:
            mix = mla_mixer(h, cos, sin, mla_w_in[j], mla_q_lat_g[j], mla_kv_lat_g[j], mla_w_uq[j],
                            mla_w_ukv[j], mla_q_norm_g[j], mla_k_norm_g[j], mla_w_o[j])
        elif m == 2:
            mix = rglru_mixer(h, lru_w_in[j], lru_conv_w[j], lru_conv_b[j], lru_w_a[j], lru_b_a[j],
                              lru_w_x[j], lru_b_x[j], lru_lam[j], lru_w_out[j])
        else:
            mix = gdn_mixer(h, gdn_w_in[j], gdn_conv_w[j], gdn_a_log[j], gdn_dt_bias[j],
                            gdn_o_norm_g[j], gdn_w_out[j])
        x = x + (1 + gate[:, 1, None, :]) * mix
        h = adaln(x, norm_g[i, 2], shift[:, 2], scale[:, 2])
        x = x + 0.5 * (1 + gate[:, 2, None, :]) * swiglu(h, ffn_w_in[i, 1], ffn_w_out[i, 1])
    return x
```

```python
from contextlib import ExitStack
import numpy as np
import concourse.bass as bass
import concourse.mybir as mybir
from concourse.bass_utils import run_bass_kernel_spmd

F32 = mybir.dt.float32
BF16 = mybir.dt.bfloat16
I32 = mybir.dt.int32
AF = mybir.ActivationFunctionType
ALU = mybir.AluOpType
EPS = 1e-6


class Cfg:
    def __init__(s, **kw):
        s.D = 4096; s.S = 2048; s.T = 512; s.DEPTH = 4; s.FFN = 5632; s.DCOND = 256
        s.MH = 32; s.QR = 1024; s.KVR = 512
        s.LB = 16
        s.GK = 16; s.GV = 32
        s.NCORES = 8; s.B = 8
        s.layers = None; s.stages = 'fmf'; s.mla_stop = 9; s.mla_cut = 0; s.debug = False
        for k, v in kw.items():
            setattr(s, k, v)
        s.DC = s.D // 128; s.FC = s.FFN // 128; s.NT = s.S // s.T
        s.GKD = s.GK * 128; s.GVD = s.GV * 128; s.GQKV = 2 * s.GKD + s.GVD
        s.GIN = s.GQKV + s.GVD + 2 * s.GV
        if s.layers is None:
            s.layers = list(range(s.DEPTH))


class Buf:
    __slots__ = ("w", "rs", "name", "excl")

    def __init__(s, name="", excl=False):
        s.w = None; s.rs = []; s.name = name; s.excl = excl


class Op:
    __slots__ = ("eng", "fn", "waits", "idx", "need_inc", "inc", "dma")


ENGS = ["pe", "act", "dve", "pool", "sp"]
NDS = 16
EPOCH = 12000


class Prog:
    def __init__(s):
        s.q = {e: [] for e in ENGS}
        s.waited = {e: {} for e in ENGS}
        s.rr = {e: 0 for e in ENGS}
        s.dcnt = {}
        s.dlast = {}

    def _dep(s, op, d, isdma):
        if d.dma is not None:
            key = ("d",) + d.dma[0]; val = d.dma[1]
        else:
            if d.eng == op.eng and not isdma and op.eng == "pe":
                return
            key = ("c", d.eng); val = d.idx
        wd = s.waited[op.eng]
        if wd.get(key, -1) >= val:
            return
        wd[key] = val
        if d.dma is None:
            d.need_inc = True
        op.waits.append(d)

    def add(s, eng, fn, r=(), w=(), dma=False):
        op = Op(); op.eng = eng; op.fn = fn; op.waits = []; op.idx = len(s.q[eng]); op.need_inc = False
        op.inc = None; op.dma = None
        for b in r:
            if b.w is not None:
                s._dep(op, b.w, dma)
            if b.excl:
                for x in b.rs:
                    if x.eng != eng:
                        s._dep(op, x, dma)
        for b in w:
            if b.w is not None:
                s._dep(op, b.w, dma)
            for x in b.rs:
                s._dep(op, x, dma)
        if dma:
            k = (eng, s.rr[eng] % NDS); s.rr[eng] += 1
            prev = s.dlast.get(k)
            if prev is not None:
                s._dep(op, prev, dma)
            c = s.dcnt.get(k, 0) + 1; s.dcnt[k] = c
            op.dma = (k, c * 16); s.dlast[k] = op
        for b in r:
            if not dma:
                b.rs = [x for x in b.rs if x.dma is not None or x.eng != eng]
            b.rs.append(op)
        for b in w:
            b.w = op; b.rs = []
        s.q[eng].append(op)
        return op

    def emit(s, nc, es):
        engsem = {}
        for e in ENGS:
            n = sum(1 for o in s.q[e] if o.need_inc)
            engsem[e] = [es.enter_context(nc.semaphore(f"s_{e}{i}")) for i in range(n // EPOCH + 1)]
            c = 0
            for o in s.q[e]:
                if o.need_inc:
                    o.inc = (engsem[e][c // EPOCH], c % EPOCH + 1); c += 1
        dsem = {}
        for k in s.dcnt:
            dsem[k] = es.enter_context(nc.semaphore(f"d_{k[0]}{k[1]}"))
        block = es.enter_context(nc.Block())

        def run(e, eng):
            for o in s.q[e]:
                ws = [(dsem[d.dma[0]], d.dma[1]) if d.dma is not None else d.inc for d in o.waits]
                for sm, v in ws[:-1]:
                    eng.wait_ge(sm, v)
                ins = o.fn(eng)
                if ws:
                    ins._wait_ge(ws[-1][0], ws[-1][1])
                if o.dma is not None:
                    ins.then_inc(dsem[o.dma[0]], 16)
                elif o.need_inc:
                    ins.then_inc(o.inc[0], 1)
            if e == "sp":
                for k, c in s.dcnt.items():
                    eng.wait_ge(dsem[k], c * 16)

        @block.tensor
        def _(eng):
            run("pe", eng)

        @block.scalar
        def _(eng):
            run("act", eng)

        @block.vector
        def _(eng):
            run("dve", eng)

        @block.gpsimd
        def _(eng):
            run("pool", eng)

        @block.sync
        def _(eng):
            run("sp", eng)


def make_consts():
    c = np.zeros((128, 1024), np.float32)
    c[:, 0:128] = np.eye(128)
    c[:, 128:256] = 1.0
    k = np.arange(64)
    c[0:64, 256:320] = (k[:, None] <= k[None, :])
    c[0:64, 320:384] = (k[:, None] == 63)
    c[0:64, 384:448] = (k[None, :] > k[:, None])
    c[0:64, 448:512] = (k[None, :] >= k[:, None])
    kk = np.arange(128)
    c[:, 512:640] = 1.0 - ((kk[:, None] >= 64) & (kk[None, :] < 64))
    R = np.zeros((64, 64), np.float32)
    for m in range(32):
        R[m, m + 32] = -1.0
        R[m + 32, m] = 1.0
    c[0:64, 640:704] = R.T
    inv = (10000.0 ** (-np.arange(0, 64, 2, dtype=np.float32) / 64)).astype(np.float32)
    c[0:64, 704] = np.concatenate([inv, inv])
    c[:, 705] = EPS
    c[:, 706] = np.pi
    c[:, 707] = 1.0
    c[:, 708] = -np.pi
    return c


class K:
    pass


def build(cfg, want_debug=False):
    D, S, T, DC, FC, NT = cfg.D, cfg.S, cfg.T, cfg.DC, cfg.FC, cfg.NT
    nc = bass.Bass("TRN2", target_bir_lowering=False)
    es = ExitStack()
    P = Prog()

    def din(name, shape, dt=F32):
        return nc.dram_tensor(name, list(shape), dt, kind="ExternalInput").ap()

    def dscr(name, shape, dt=F32):
        return nc.dram_tensor(name, list(shape), dt, kind="Internal").ap()

    xT = din("xT", [D, S])
    cvec = din("cvec", [128, DC])
    pos = din("positions", [1, S], I32)
    consts = din("consts", [128, 1024])
    cond_w = din("cond_w", [D, cfg.DCOND])
    cond_b = din("cond_b", [128, cfg.DCOND // 128])
    mod_w = din("mod_w", [cfg.DEPTH, cfg.DCOND, 9 * D])
    mod_b = din("mod_b", [cfg.DEPTH, 128, 9 * DC])
    norm_g = din("norm_g", [cfg.DEPTH, 128, 3 * DC])
    ffn_w_in = din("ffn_w_in", [cfg.DEPTH, 2, D, 2 * cfg.FFN])
    ffn_w_out = din("ffn_w_out", [cfg.DEPTH, 2, cfg.FFN, D])
    sc_w_in = din("sc_w_in", [D, 3 * D])
    sc_conv_w = din("sc_conv_w", [128, 3 * DC])
    sc_w_out = din("sc_w_out", [D, D])
    mla_w_in = din("mla_w_in", [D, cfg.QR + cfg.KVR + 64])
    mla_lat_g = din("mla_lat_g", [128, (cfg.QR + cfg.KVR) // 128])
    mla_w_uq = din("mla_w_uq", [cfg.QR, cfg.MH * 192])
    mla_w_ukv = din("mla_w_ukv", [cfg.KVR, cfg.MH * 256])
    mla_qk_g = din("mla_qk_g", [128, 4])
    mla_w_o = din("mla_w_o", [cfg.MH * 128, D])
    lru_w_in = din("lru_w_in", [D, 2 * D])
    lru_vec = din("lru_vec", [128, 8 * DC])
    lru_w_a = din("lru_w_a", [cfg.LB, 256, 256])
    lru_w_x = din("lru_w_x", [cfg.LB, 256, 256])
    lru_w_out = din("lru_w_out", [D, D])
    gdn_w_in = din("gdn_w_in", [D, cfg.GIN])
    gdn_conv_w = din("gdn_conv_w", [128, 4 * (cfg.GQKV // 128)])
    gdn_hv = din("gdn_hv", [64, 2])
    gdn_o_g = din("gdn_o_g", [128, 1])
    gdn_w_out = din("gdn_w_out", [cfg.GVD, D])
    yT = nc.dram_tensor("yT", [D, S], F32, kind="ExternalOutput").ap()
    xs = [dscr("xs0", [D, S]), dscr("xs1", [D, S])]

    def sb(name, shape, dt=F32):
        return es.enter_context(nc.sbuf_tensor(name, list(shape), dt))

    es.enter_context(nc.allow_low_precision("bf16 matmul per problem tolerance"))
    cst = sb("cst", [128, 1024]); cstB = Buf("cst")
    cstb = sb("cstb", [128, 1024], BF16)
    hT = sb("hT", [128, DC, T], BF16); hB = [Buf(f"h{i}") for i in range(DC)]
    NA2 = max(FC, 44)
    a2 = sb("a2", [128, NA2, T], BF16); a2B = [Buf(f"a2_{i}") for i in range(NA2)]
    NW = 4
    KCMAX = max(FC, DC)
    WSZ = KCMAX * 128
    wflat = sb("wflat", [128, NW * WSZ], BF16); wB = [Buf(f"w{i}") for i in range(NW)]
    wsl = wflat[:, :].rearrange("p (s k m) -> p s k m", s=NW, m=128)

    def wpair(a, KC):
        return wflat[:, 2 * a * WSZ:2 * a * WSZ + KC * 256].rearrange("p (k m) -> p k m", m=256)
    NX = 4
    xin = sb("xin", [128, NX, T]); xinB = [Buf(f"xin{i}") for i in range(NX)]
    NTMP = 10
    tmp = sb("tmp", [128, NTMP, T + 8]); tmpB = [Buf(f"tmp{i}") for i in range(NTMP)]
    MXW = 9216
    mx = sb("mx", [128, MXW])

    def carve(off, n, dt=F32, parts=128):
        return mx[0:parts, off:off + n] if dt == F32 else mx[0:parts, off:off + n].bitcast(dt)

    _mo = cfg.DEPTH * 9 * DC
    modv = carve(0, _mo).rearrange("p (l f) -> p l f", f=9 * DC); modB = Buf("mod")
    drv = sb("drv", [128, cfg.DEPTH, 9 * DC]); drvB = Buf("drv")
    ps = es.enter_context(nc.psum_tensor("ps", [128, 8, 512], F32)); psB = [Buf(f"ps{i}", excl=True) for i in range(8)]
    st = {"ps": 0, "tmp": 0, "w": 0, "xin": 0, "wp": 0}

    ident = cst[:, 0:128]; ones = cst[:, 128:256]
    identb = cstb[:, 0:128]; onesb = cstb[:, 128:256]
    epsc = cst[:, 705:706]

    def nps():
        i = st["ps"]; st["ps"] = (i + 1) % 6
        return i

    def ntmp():
        i = st["tmp"]; st["tmp"] = (i + 1) % NTMP
        return i

    def nw():
        i = st["w"]; st["w"] = (i + 1) % NW
        return i

    def nxin():
        i = st["xin"]; st["xin"] = (i + 1) % NX
        return i

    dramB = {}

    def dB(name, t):
        k = (name, t)
        if k not in dramB:
            dramB[k] = Buf(str(k))
        return dramB[k]

    P.add("sp", lambda e: e.dma_start(out=cst[:, :], in_=consts), w=[cstB], dma=True)
    P.add("act", lambda e: e.activation(out=cstb[:, :], in_=cst[:, :], func=AF.Copy), r=[cstB], w=[cstB])

    def modulation():
        JC = cfg.DCOND // 128
        big = wsl
        cv = sb("cv", [128, DC]); cvB = Buf("cv")
        cbv = sb("cbv", [128, JC]); cond = sb("cond", [128, JC]); condB = Buf("cond"); st['cond'] = cond
        P.add("sp", lambda e: e.dma_start(out=cv[:, :], in_=cvec), w=[cvB], dma=True)
        P.add("sp", lambda e: e.dma_start(out=cbv[:, :], in_=cond_b), w=[cvB], dma=True)
        pi = nps()
        cwv = cond_w.rearrange("(kc p) j -> p kc j", p=128)
        for jc in range(JC):
            for kc in range(DC):
                ti = ntmp()
                P.add("sp", lambda e, ti=ti, kc=kc: e.dma_start(out=tmp[:, ti, 0:cfg.DCOND], in_=cwv[:, kc, :]),
                      w=[tmpB[ti]], dma=True)
                P.add("pe", lambda e, ti=ti, kc=kc, jc=jc, pi=pi: e.matmul(
                    ps[:, pi, jc:jc + 1], tmp[:, ti, jc * 128:(jc + 1) * 128], cv[:, kc:kc + 1],
                    start=(kc == 0), stop=(kc == DC - 1)), r=[tmpB[ti], cvB], w=[psB[pi]])
        for jc in range(JC):
            P.add("act", lambda e, jc=jc, pi=pi: e.activation(out=cond[:, jc:jc + 1], in_=ps[:, pi, jc:jc + 1], func=AF.Silu,
                                                      bias=cbv[:, jc:jc + 1]), r=[psB[pi], cvB], w=[condB])
        mb = carve(_mo, _mo).rearrange("p (l f) -> p l f", f=9 * DC); mbB = Buf("mb")
        ng = carve(2 * _mo, cfg.DEPTH * 3 * DC).rearrange("p (l f) -> p l f", f=3 * DC)
        P.add("sp", lambda e: e.dma_start(out=mb[:, :, :], in_=mod_b.rearrange("l p f -> p l f")), w=[mbB], dma=True)
        P.add("sp", lambda e: e.dma_start(out=ng[:, :, :], in_=norm_g.rearrange("l p f -> p l f")), w=[mbB], dma=True)
        for l in cfg.layers:
            pi = nps()
            mwv = mod_w[l].rearrange("(jc p) f -> p jc f", p=128)
            NG = 9 * DC
            GS = 2
            assert JC * GS * 128 <= T + 8
            for g0 in range(0, NG, GS):
                ti = ntmp()
                gn = min(GS, NG - g0)
                for jc in range(JC):
                    P.add("sp", lambda e, ti=ti, jc=jc, g0=g0, gn=gn, mwv=mwv: e.dma_start(
                        out=tmp[:, ti, jc * GS * 128: jc * GS * 128 + gn * 128],
                        in_=mwv[:, jc, g0 * 128:(g0 + gn) * 128]), w=[tmpB[ti]], dma=True)
                for g in range(gn):
                    for jc in range(JC):
                        P.add("pe", lambda e, ti=ti, jc=jc, g=g, g0=g0, pi=pi: e.matmul(
                            ps[:, pi, g0 + g:g0 + g + 1],
                            tmp[:, ti, jc * GS * 128 + g * 128: jc * GS * 128 + (g + 1) * 128],
                            cond[:, jc:jc + 1], start=(jc == 0), stop=(jc == JC - 1)),
                            r=[tmpB[ti], condB], w=[psB[pi]])
            P.add("dve", lambda e, l=l, pi=pi: e.tensor_tensor(out=modv[:, l, :], in0=ps[:, pi, 0:9 * DC], in1=mb[:, l, :],
                                                               op=ALU.add), r=[psB[pi], mbB], w=[modB])
            for sub in range(3):
                o = sub * 3 * DC
                sh = modv[:, l, o:o + DC]; sc = modv[:, l, o + DC:o + 2 * DC]; gt = modv[:, l, o + 2 * DC:o + 3 * DC]
                P.add("dve", lambda e, l=l, sub=sub, sc=sc, o=o: e.scalar_tensor_tensor(
                    out=drv[:, l, o:o + DC], in0=sc, scalar=1.0, in1=ng[:, l, sub * DC:(sub + 1) * DC],
                    op0=ALU.add, op1=ALU.mult), r=[modB, mbB], w=[drvB])
                P.add("dve", lambda e, l=l, sh=sh, o=o: e.tensor_copy(out=drv[:, l, o + DC:o + 2 * DC], in_=sh),
                      r=[modB], w=[drvB])
                cf = 1.0 if sub == 1 else 0.5
                P.add("dve", lambda e, l=l, gt=gt, o=o, cf=cf: e.tensor_scalar(
                    out=drv[:, l, o + 2 * DC:o + 3 * DC], in0=gt, scalar1=1.0, scalar2=cf, op0=ALU.add, op1=ALU.mult),
                    r=[modB], w=[drvB])

    def Avec(l, sub, c):
        o = sub * 3 * DC
        return drv[:, l, o + c:o + c + 1]

    def Svec(l, sub, c):
        o = sub * 3 * DC + DC
        return drv[:, l, o + c:o + c + 1]

    def Gvec(l, sub, c):
        o = sub * 3 * DC + 2 * DC
        return drv[:, l, o + c:o + c + 1]

    def load_x(xsrc, xname, c, t):
        xi = nxin()
        P.add("sp", lambda e: e.dma_start(out=xin[:, xi, :], in_=xsrc[c * 128:(c + 1) * 128, t * T:(t + 1) * T]),
              r=[dB(xname, t)], w=[xinB[xi]], dma=True)
        return xi

    def rstd_from_chunks(chunk_aps, chunk_bufs, nfeat, parts=None, eps_ap=None):
        pi = nps()
        n = len(chunk_aps)
        for i, (ap, bf, npart) in enumerate(chunk_aps):
            ti = ntmp()
            P.add("act", lambda e, ap=ap, ti=ti, npart=npart: e.activation(
                out=sqb[0:npart, ti % 4, :], in_=ap, func=AF.Square), r=[bf], w=[sqbB[ti % 4]])
            P.add("pe", lambda e, ti=ti, npart=npart, i=i: e.matmul(ps[:, pi, 0:T], onesb[0:npart, :], sqb[0:npart, ti % 4, :],
                                                                      start=(i == 0), stop=(i == n - 1)),
                  r=[sqbB[ti % 4], cstB], w=[psB[pi]])
        ri = ntmp()
        P.add("act", lambda e: e.activation(out=tmp[:, ri, 0:T], in_=ps[:, pi, 0:T], func=AF.Sqrt, bias=epsc,
                                            scale=1.0 / nfeat), r=[psB[pi], cstB], w=[tmpB[ri]])
        P.add("dve", lambda e: e.reciprocal(out=tmp[:, ri, 0:T], in_=tmp[:, ri, 0:T]), r=[tmpB[ri]], w=[tmpB[ri]])
        return ri

    sqb = sb("sqb", [128, 4, T], BF16); sqbB = [Buf(f"sqb{i}") for i in range(4)]
    rstdt = sb("rstdt", [128, T]); rstdB = Buf("rstd")

    def adaln(xsrc, xname, l, sub, t):
        chunks = []
        pi = nps()
        for c in range(DC):
            xi = load_x(xsrc, xname, c, t)
            k = c % 4
            P.add("act", lambda e, xi=xi, k=k: e.activation(out=sqb[:, k, :], in_=xin[:, xi, :], func=AF.Square),
                  r=[xinB[xi]], w=[sqbB[k]])
            P.add("pe", lambda e, k=k, c=c: e.matmul(ps[:, pi, 0:T], onesb, sqb[:, k, :], start=(c == 0), stop=(c == DC - 1)),
                  r=[sqbB[k], cstB], w=[psB[pi]])
        P.add("act", lambda e: e.activation(out=rstdt[:, :], in_=ps[:, pi, 0:T], func=AF.Sqrt, bias=epsc, scale=1.0 / D),
              r=[psB[pi], cstB], w=[rstdB])
        P.add("dve", lambda e: e.reciprocal(out=rstdt[:, :], in_=rstdt[:, :]), r=[rstdB], w=[rstdB])
        for c in range(DC):
            xi = load_x(xsrc, xname, c, t)
            ti = ntmp()
            P.add("dve", lambda e, xi=xi, ti=ti: e.tensor_tensor(out=tmp[:, ti, 0:T], in0=xin[:, xi, :], in1=rstdt[:, :],
                                                                 op=ALU.mult), r=[xinB[xi], rstdB], w=[tmpB[ti]])
            P.add("act", lambda e, ti=ti, c=c: e.activation(out=hT[:, c, :], in_=tmp[:, ti, 0:T], func=AF.Identity,
                                                            bias=Svec(l, sub, c), scale=Avec(l, sub, c)),
                  r=[tmpB[ti], drvB], w=[hB[c]])

    def load_w(W, KC, c0, ncol=128, krows=None):
        wi = nw()
        wv = W.rearrange("(kc p) m -> p kc m", p=128)
        P.add("pool", lambda e: e.dma_start(out=wsl[:, wi, 0:KC, 0:ncol], in_=wv[:, :, c0:c0 + ncol]), w=[wB[wi]], dma=True)
        return wi

    def lin_chunk(W, KC, c0, in_ap, in_bufs, ncol=128, tcols=T):
        wi = load_w(W, KC, c0, ncol)
        pi = nps()
        for kc in range(KC):
            P.add("pe", lambda e, kc=kc: e.matmul(ps[0:ncol, pi, 0:tcols], wsl[:, wi, kc, 0:ncol], in_ap(kc),
                                                  start=(kc == 0), stop=(kc == KC - 1)),
                  r=[wB[wi], in_bufs[kc]], w=[psB[pi]])
        return pi

    def lin_pair(W, KC, c0, in_ap, in_bufs):
        a = st["wp"]; st["wp"] = 1 - a
        wv = W.rearrange("(kc p) m -> p kc m", p=128)
        wp_ = wpair(a, KC)
        P.add("pool", lambda e: e.dma_start(out=wp_, in_=wv[:, :, c0:c0 + 256]), w=[wB[2 * a], wB[2 * a + 1]], dma=True)
        pis = []
        for s_ in range(2):
            pi = nps()
            for kc in range(KC):
                P.add("pe", lambda e, kc=kc, pi=pi, s_=s_: e.matmul(ps[:, pi, 0:T], wp_[:, kc, s_ * 128:(s_ + 1) * 128], in_ap(kc),
                                                                  start=(kc == 0), stop=(kc == KC - 1)),
                      r=[wB[2 * a], wB[2 * a + 1], in_bufs[kc]], w=[psB[pi]])
            pis.append(pi)
        return pis

    def residual(pi, xsrc, xname, xdst, dname, l, sub, c, t):
        xi = load_x(xsrc, xname, c, t)
        P.add("dve", lambda e: e.scalar_tensor_tensor(out=xin[:, xi, :], in0=ps[:, pi, 0:T], scalar=Gvec(l, sub, c),
                                                      in1=xin[:, xi, :], op0=ALU.mult, op1=ALU.add),
              r=[psB[pi], drvB, xinB[xi]], w=[xinB[xi]])
        P.add("sp", lambda e: e.dma_start(out=xdst[c * 128:(c + 1) * 128, t * T:(t + 1) * T], in_=xin[:, xi, :]),
              r=[xinB[xi]], w=[dB(dname, t)], dma=True)

    def out_proj(W, KC, in_ap, in_bufs, xsrc, xname, xdst, dname, l, sub, t):
        for c in range(0, DC - 1, 2):
            pis = lin_pair(W, KC, c * 128, in_ap, in_bufs)
            for s_ in range(2):
                residual(pis[s_], xsrc, xname, xdst, dname, l, sub, c + s_, t)
        if DC % 2:
            c = DC - 1
            pi = lin_chunk(W, KC, c * 128, in_ap, in_bufs)
            residual(pi, xsrc, xname, xdst, dname, l, sub, c, t)

    def ffn(l, which, sub, xsrc, xname, xdst, dname):
        Win = ffn_w_in[l, which]; Wout = ffn_w_out[l, which]
        for t in range(NT):
            adaln(xsrc, xname, l, sub, t)
            def gate_up(pg, pu, j):
                ti = ntmp()
                P.add("act", lambda e: e.activation(out=tmp[:, ti, 0:T], in_=ps[:, pg, 0:T], func=AF.Silu),
                      r=[psB[pg]], w=[tmpB[ti]])
                P.add("dve", lambda e: e.tensor_tensor(out=a2[:, j, :], in0=ps[:, pu, 0:T], in1=tmp[:, ti, 0:T], op=ALU.mult),
                      r=[psB[pu], tmpB[ti]], w=[a2B[j]])
            hin_ = lambda kc: hT[:, kc, :]
            for j in range(0, FC - 1, 2):
                pgs = lin_pair(Win, DC, j * 128, hin_, hB)
                pus = lin_pair(Win, DC, cfg.FFN + j * 128, hin_, hB)
                for s_ in range(2):
                    gate_up(pgs[s_], pus[s_], j + s_)
            if FC % 2:
                j = FC - 1
                pg = lin_chunk(Win, DC, j * 128, hin_, hB)
                pu = lin_chunk(Win, DC, cfg.FFN + j * 128, hin_, hB)
                gate_up(pg, pu, j)
            out_proj(Wout, FC, lambda kc: a2[:, kc, :], a2B, xsrc, xname, xdst, dname, l, sub, t)

    def conv_chunk(zt, hal, halB, c, KW, wap, bias_ap, t):
        H = KW - 1
        if t == 0:
            P.add("dve", lambda e: e.memset(tmp[:, zt, 0:H], 0.0), w=[tmpB[zt]])
        else:
            P.add("dve", lambda e: e.tensor_copy(out=tmp[:, zt, 0:H], in_=hal[:, c, 0:H]), r=[halB], w=[tmpB[zt]])
        P.add("dve", lambda e: e.tensor_copy(out=hal[:, c, 0:H], in_=tmp[:, zt, T:T + H]), r=[tmpB[zt]], w=[halB])
        oi = ntmp()
        if bias_ap is None:
            P.add("dve", lambda e: e.tensor_scalar(out=tmp[:, oi, 0:T], in0=tmp[:, zt, 0:T], scalar1=wap(0), scalar2=None,
                                                   op0=ALU.mult), r=[tmpB[zt], vecB], w=[tmpB[oi]])
        else:
            P.add("dve", lambda e: e.tensor_scalar(out=tmp[:, oi, 0:T], in0=tmp[:, zt, 0:T], scalar1=wap(0), scalar2=bias_ap,
                                                   op0=ALU.mult, op1=ALU.add), r=[tmpB[zt], vecB], w=[tmpB[oi]])
        for j in range(1, KW):
            P.add("dve", lambda e, j=j: e.scalar_tensor_tensor(out=tmp[:, oi, 0:T], in0=tmp[:, zt, j:j + T], scalar=wap(j),
                                                               in1=tmp[:, oi, 0:T], op0=ALU.mult, op1=ALU.add),
                  r=[tmpB[zt], tmpB[oi], vecB], w=[tmpB[oi]])
        return oi

    NVEC = max(3 * DC, 8 * DC, 4 * (cfg.GQKV // 128))
    vec = sb("vec", [128, NVEC]); vecB = Buf("vec")
    NHAL = max(DC, cfg.GQKV // 128)
    hal = sb("hal", [128, NHAL, 4]); halB = Buf("hal")

    def sconv(l, xsrc, xname, xdst, dname):
        P.add("sp", lambda e: e.dma_start(out=vec[:, 0:3 * DC], in_=sc_conv_w), w=[vecB], dma=True)
        for t in range(NT):
            adaln(xsrc, xname, l, 1, t)
            for c in range(DC):
                hin = lambda kc: hT[:, kc, :]
                pb = lin_chunk(sc_w_in, DC, c * 128, hin, hB)
                pc = lin_chunk(sc_w_in, DC, D + c * 128, hin, hB)
                px = lin_chunk(sc_w_in, DC, 2 * D + c * 128, hin, hB)
                xi = ntmp()
                P.add("act", lambda e, xi=xi, px=px: e.activation(out=tmp[:, xi, 0:T], in_=ps[:, px, 0:T], func=AF.Copy),
                      r=[psB[px]], w=[tmpB[xi]])
                zt = ntmp()
                P.add("dve", lambda e, zt=zt, pc=pc, xi=xi: e.tensor_tensor(out=tmp[:, zt, 2:2 + T], in0=ps[:, pc, 0:T],
                                                                            in1=tmp[:, xi, 0:T], op=ALU.mult),
                      r=[psB[pc], tmpB[xi]], w=[tmpB[zt]])
                oi = conv_chunk(zt, hal, halB, c, 3, lambda j, c=c: vec[:, j * DC + c:j * DC + c + 1], None, t)
                P.add("dve", lambda e, oi=oi, pb=pb, c=c: e.tensor_tensor(out=a2[:, c, :], in0=ps[:, pb, 0:T],
                                                                          in1=tmp[:, oi, 0:T], op=ALU.mult),
                      r=[psB[pb], tmpB[oi]], w=[a2B[c]])
            out_proj(sc_w_out, DC, lambda kc: a2[:, kc, :], a2B, xsrc, xname, xdst, dname, l, 1, t)

    def rglru(l, xsrc, xname, xdst, dname):
        LB = cfg.LB
        P.add("sp", lambda e: e.dma_start(out=vec[:, 0:8 * DC], in_=lru_vec), w=[vecB], dma=True)
        wblk = carve(0, 1024, BF16).rearrange("p (a b c d) -> p a b c d", a=2, b=2, c=2); wblkBs = [Buf("wblk0"), Buf("wblk1")]
        lst = carve(1024, DC); lstB = Buf("lst")
        lsc = carve(1024 + DC, DC); lscB = Buf("lsc")
        lam = vec[:, 7 * DC:8 * DC]
        la = carve(1024 + 2 * DC, DC); lb_ = carve(1024 + 3 * DC, DC)
        P.add("act", lambda e: e.activation(out=la[:, :], in_=lam, func=AF.Abs), r=[vecB], w=[lscB])
        P.add("act", lambda e: e.activation(out=la[:, :], in_=la[:, :], func=AF.Exp, scale=-1.0), r=[lscB], w=[lscB])
        P.add("act", lambda e: e.activation(out=la[:, :], in_=la[:, :], func=AF.Ln, bias=cst[:, 707:708]), r=[lscB, cstB], w=[lscB])
        P.add("dve", lambda e: e.tensor_scalar(out=lb_[:, :], in0=lam, scalar1=-1.0, scalar2=0.0, op0=ALU.mult, op1=ALU.max),
              r=[vecB], w=[lstB])
        P.add("dve", lambda e: e.tensor_tensor(out=la[:, :], in0=la[:, :], in1=lb_[:, :], op=ALU.add), r=[lscB, lstB], w=[lscB])
        P.add("dve", lambda e: e.tensor_scalar(out=lsc[:, :], in0=la[:, :], scalar1=-8.0, scalar2=None, op0=ALU.mult),
              r=[lscB], w=[lscB])
        P.add("dve", lambda e: e.memset(lst[:, :], 0.0), r=[lstB], w=[lstB])
        xrb = carve(1024 + 4 * DC, T, BF16).rearrange("p (a b) -> p a b", a=2); xrbB = [Buf("xrb0"), Buf("xrb1")]
        for t in range(NT):
            adaln(xsrc, xname, l, 1, t)
            for n in range(LB):
                hin = lambda kc: hT[:, kc, :]
                xr = []
                wp = n % 2
                wblkB = wblkBs[wp]
                for gi, Wb in enumerate((lru_w_a, lru_w_x)):
                    P.add("pool", lambda e, gi=gi, Wb=Wb, n=n, wp=wp: e.dma_start(
                        out=wblk[:, wp, gi, :, :], in_=Wb[n].rearrange("(dc p) e -> p dc e", p=128)), w=[wblkB], dma=True)
                for k in range(2):
                    c = 2 * n + k
                    px = lin_chunk(lru_w_in, DC, D + c * 128, hin, hB)
                    zt = ntmp()
                    P.add("act", lambda e, zt=zt, px=px: e.activation(out=tmp[:, zt, 3:3 + T], in_=ps[:, px, 0:T], func=AF.Copy),
                          r=[psB[px]], w=[tmpB[zt]])
                    oi = conv_chunk(zt, hal, halB, c, 4, lambda j, c=c: vec[:, j * DC + c:j * DC + c + 1],
                                    vec[:, 4 * DC + c:4 * DC + c + 1], t)
                    P.add("act", lambda e, oi=oi, k=k: e.activation(out=xrb[:, k, :], in_=tmp[:, oi, 0:T], func=AF.Copy),
                          r=[tmpB[oi]], w=[xrbB[k]])
                    xr.append(oi)
                for k in range(2):
                    c = 2 * n + k
                    pg = lin_chunk(lru_w_in, DC, c * 128, hin, hB)
                    g1 = ntmp()
                    P.add("act", lambda e, g1=g1, pg=pg: e.activation(out=tmp[:, g1, 0:T], in_=ps[:, pg, 0:T], func=AF.Square),
                          r=[psB[pg]], w=[tmpB[g1]])
                    P.add("dve", lambda e, g1=g1: e.tensor_scalar(out=tmp[:, g1, 0:T], in0=tmp[:, g1, 0:T], scalar1=0.044715,
                                                                  scalar2=1.0, op0=ALU.mult, op1=ALU.add), r=[tmpB[g1]], w=[tmpB[g1]])
                    P.add("dve", lambda e, g1=g1, pg=pg: e.tensor_tensor(out=tmp[:, g1, 0:T], in0=ps[:, pg, 0:T], in1=tmp[:, g1, 0:T],
                                                                         op=ALU.mult), r=[psB[pg], tmpB[g1]], w=[tmpB[g1]])
                    P.add("act", lambda e, g1=g1: e.activation(out=tmp[:, g1, 0:T], in_=tmp[:, g1, 0:T], func=AF.Sigmoid,
                                                               scale=1.5957691216), r=[tmpB[g1]], w=[tmpB[g1]])
                    P.add("dve", lambda e, g1=g1, pg=pg: e.tensor_tensor(out=tmp[:, g1, 0:T], in0=ps[:, pg, 0:T], in1=tmp[:, g1, 0:T],
                                                                         op=ALU.mult), r=[psB[pg], tmpB[g1]], w=[tmpB[g1]])
                    pa = nps(); pxx = nps()
                    for gi, pp in ((0, pa), (1, pxx)):
                        for dc in range(2):
                            P.add("pe", lambda e, gi=gi, pp=pp, dc=dc, k=k, wp=wp: e.matmul(
                                ps[:, pp, 0:T], wblk[:, wp, gi, dc, k * 128:(k + 1) * 128], xrb[:, dc, :],
                                start=(dc == 0), stop=(dc == 1)), r=[wblkB, xrbB[dc]], w=[psB[pp]])
                    r_ = ntmp(); i_ = ntmp()
                    P.add("act", lambda e, r_=r_, pa=pa, c=c: e.activation(out=tmp[:, r_, 0:T], in_=ps[:, pa, 0:T], func=AF.Sigmoid,
                                                                           bias=vec[:, 5 * DC + c:5 * DC + c + 1]),
                          r=[psB[pa], vecB], w=[tmpB[r_]])
                    P.add("act", lambda e, i_=i_, pxx=pxx, c=c: e.activation(out=tmp[:, i_, 0:T], in_=ps[:, pxx, 0:T], func=AF.Sigmoid,
                                                                             bias=vec[:, 6 * DC + c:6 * DC + c + 1]),
                          r=[psB[pxx], vecB], w=[tmpB[i_]])
                    P.add("act", lambda e, r_=r_, c=c: e.activation(out=tmp[:, r_, 0:T], in_=tmp[:, r_, 0:T], func=AF.Exp,
                                                                    scale=lsc[:, c:c + 1]), r=[tmpB[r_], lscB], w=[tmpB[r_]])
                    m_ = ntmp()
                    P.add("dve", lambda e, r_=r_, m_=m_: e.tensor_tensor(out=tmp[:, m_, 0:T], in0=tmp[:, r_, 0:T], in1=tmp[:, r_, 0:T],
                                                                         op=ALU.mult), r=[tmpB[r_]], w=[tmpB[m_]])
                    P.add("dve", lambda e, m_=m_: e.tensor_scalar(out=tmp[:, m_, 0:T], in0=tmp[:, m_, 0:T], scalar1=-1.0, scalar2=1.0,
                                                                  op0=ALU.mult, op1=ALU.add), r=[tmpB[m_]], w=[tmpB[m_]])
                    P.add("act", lambda e, m_=m_: e.activation(out=tmp[:, m_, 0:T], in_=tmp[:, m_, 0:T], func=AF.Sqrt),
                          r=[tmpB[m_]], w=[tmpB[m_]])
                    P.add("dve", lambda e, i_=i_, k=k, xr=xr: e.tensor_tensor(out=tmp[:, i_, 0:T], in0=tmp[:, i_, 0:T],
                                                                              in1=tmp[:, xr[k], 0:T], op=ALU.mult),
                          r=[tmpB[i_], tmpB[xr[k]]], w=[tmpB[i_]])
                    P.add("dve", lambda e, i_=i_, m_=m_: e.tensor_tensor(out=tmp[:, i_, 0:T], in0=tmp[:, i_, 0:T], in1=tmp[:, m_, 0:T],
                                                                         op=ALU.mult), r=[tmpB[i_], tmpB[m_]], w=[tmpB[i_]])
                    P.add("dve", lambda e, r_=r_, i_=i_, m_=m_, c=c: e.tensor_tensor_scan(
                        out=tmp[:, m_, 0:T], data0=tmp[:, r_, 0:T], data1=tmp[:, i_, 0:T], initial=lst[:, c:c + 1],
                        op0=ALU.mult, op1=ALU.add), r=[tmpB[r_], tmpB[i_], lstB], w=[tmpB[m_]])
                    P.add("dve", lambda e, m_=m_, c=c: e.tensor_copy(out=lst[:, c:c + 1], in_=tmp[:, m_, T - 1:T]),
                          r=[tmpB[m_]], w=[lstB])
                    P.add("dve", lambda e, m_=m_, g1=g1, c=c: e.tensor_tensor(out=a2[:, c, :], in0=tmp[:, m_, 0:T], in1=tmp[:, g1, 0:T],
                                                                              op=ALU.mult), r=[tmpB[m_], tmpB[g1]], w=[a2B[c]])
            out_proj(lru_w_out, DC, lambda kc: a2[:, kc, :], a2B, xsrc, xname, xdst, dname, l, 1, t)


    def mla(l, xsrc, xname, xdst, dname):
        MH, QR, KVR = cfg.MH, cfg.QR, cfg.KVR
        QRC, KVC = QR // 128, KVR // 128
        assert QRC <= 8 and KVC <= 4 and MH <= DC
        SC = 192.0 ** -0.5
        kT_scr = dscr("kT_scr", [MH, 192, S], BF16)
        V_scr = dscr("V_scr", [MH, S, 128], BF16)
        mlaf = carve(0, 4 * T).rearrange("p (a b) -> p a b", a=4); mlafB = [Buf(f"mlaf{i}") for i in range(4)]
        posi = carve(4 * T, T, I32, 64)
        kint = carve(5 * T, T, I32, 64); kintB = Buf("kint")
        gl = carve(6 * T, QRC + KVC + 4); glB = Buf("gl")
        P.add("sp", lambda e: e.dma_start(out=gl[:, 0:QRC + KVC], in_=mla_lat_g), w=[glB], dma=True)
        P.add("sp", lambda e: e.dma_start(out=gl[:, QRC + KVC:QRC + KVC + 4], in_=mla_qk_g), w=[glB], dma=True)
        gq_n = gl[:, QRC + KVC:QRC + KVC + 1]; gq_r = gl[0:64, QRC + KVC + 1:QRC + KVC + 2]
        gk_n = gl[:, QRC + KVC + 2:QRC + KVC + 3]; gk_r = gl[0:64, QRC + KVC + 3:QRC + KVC + 4]
        RT = cst[0:64, 640:704]; inv2 = cst[0:64, 704:705]; mpi = cst[0:64, 708:709]
        maskd = cstb[:, 512:640]
        bst = {"i": 0}

        def nbs():
            i = 36 + bst["i"]; bst["i"] = (bst["i"] + 1) % 8
            return i

        def rope(tq, out_ap, rstd_ap):
            pr = nps()
            P.add("pe", lambda e: e.matmul(ps[0:64, pr, 0:T], RT, tmp[0:64, tq, 0:T], start=True, stop=True),
                  r=[cstB, tmpB[tq]], w=[psB[pr]])
            t2 = ntmp()
            P.add("dve", lambda e: e.tensor_tensor(out=tmp[0:64, t2, 0:T], in0=ps[0:64, pr, 0:T], in1=mlaf[0:64, 1, :], op=ALU.mult),
                  r=[psB[pr], mlafB[1]], w=[tmpB[t2]])
            P.add("dve", lambda e: e.tensor_tensor(out=tmp[0:64, tq, 0:T], in0=tmp[0:64, tq, 0:T], in1=mlaf[0:64, 0, :], op=ALU.mult),
                  r=[tmpB[tq], mlafB[0]], w=[tmpB[tq]])
            if rstd_ap is None:
                P.add("dve", lambda e: e.tensor_tensor(out=out_ap, in0=tmp[0:64, tq, 0:T], in1=tmp[0:64, t2, 0:T], op=ALU.add),
                      r=[tmpB[tq], tmpB[t2]], w=[mlafB[2]])
            else:
                P.add("dve", lambda e: e.tensor_tensor(out=tmp[0:64, tq, 0:T], in0=tmp[0:64, tq, 0:T], in1=tmp[0:64, t2, 0:T], op=ALU.add),
                      r=[tmpB[tq], tmpB[t2]], w=[tmpB[tq]])

        def mla_tile(t):
            if cfg.mla_cut == 10:
                return
            adaln(xsrc, xname, l, 1, t)
            if cfg.mla_cut == 11:
                return
            hin = lambda kc: hT[:, kc, :]
            pq = 6; pkv = 7
            for c in range(QRC + KVC):
                pi = lin_chunk(mla_w_in, DC, c * 128, hin, hB)
                k = c % 4
                P.add("act", lambda e, pi=pi, k=k: e.activation(out=sqb[:, k, :], in_=ps[:, pi, 0:T], func=AF.Square),
                      r=[psB[pi]], w=[sqbB[k]])
                pp = pq if c < QRC else pkv
                first = (c == 0 or c == QRC); last = (c == QRC - 1 or c == QRC + KVC - 1)
                P.add("pe", lambda e, k=k, pp=pp, first=first, last=last: e.matmul(ps[:, pp, 0:T], onesb, sqb[:, k, :],
                                                                                   start=first, stop=last),
                      r=[sqbB[k], cstB], w=[psB[pp]])
                P.add("dve", lambda e, pi=pi, c=c: e.tensor_copy(out=a2[:, 12 + c, :], in_=ps[:, pi, 0:T]), r=[psB[pi]], w=[a2B[12 + c]])
            if cfg.mla_cut == 1:
                return
            ppe = lin_chunk(mla_w_in, DC, QR + KVR, hin, hB, ncol=64)
            if cfg.mla_cut == 2:
                return
            rq = ntmp(); rkv = ntmp()
            for rr, pp, nf in ((rq, pq, QR), (rkv, pkv, KVR)):
                P.add("act", lambda e, rr=rr, pp=pp, nf=nf: e.activation(out=tmp[:, rr, 0:T], in_=ps[:, pp, 0:T], func=AF.Sqrt, bias=epsc,
                                                                         scale=1.0 / nf), r=[psB[pp], cstB], w=[tmpB[rr]])
                P.add("dve", lambda e, rr=rr: e.reciprocal(out=tmp[:, rr, 0:T], in_=tmp[:, rr, 0:T]), r=[tmpB[rr]], w=[tmpB[rr]])
            for c in range(QRC + KVC):
                rr = rq if c < QRC else rkv
                dst = c if c < QRC else 8 + (c - QRC)
                P.add("dve", lambda e, c=c, rr=rr, dst=dst: e.scalar_tensor_tensor(
                    out=a2[:, dst, :], in0=a2[:, 12 + c, :], scalar=gl[:, c:c + 1], in1=tmp[:, rr, 0:T], op0=ALU.mult, op1=ALU.mult),
                    r=[a2B[12 + c], glB, tmpB[rr]], w=[a2B[dst]])
            if cfg.mla_cut == 3:
                return
            P.add("sp", lambda e, t=t: e.dma_start(out=posi[:, :], in_=pos[0:1, t * T:(t + 1) * T].partition_broadcast(64)),
                  w=[mlafB[3]], dma=True)
            P.add("dve", lambda e: e.tensor_copy(out=mlaf[0:64, 3, :], in_=posi[:, :]), r=[mlafB[3]], w=[mlafB[3]])
            P.add("dve", lambda e: e.tensor_scalar(out=mlaf[0:64, 3, :], in0=mlaf[0:64, 3, :], scalar1=inv2, scalar2=None, op0=ALU.mult),
                  r=[mlafB[3], cstB], w=[mlafB[3]])
            C1 = 6.28125; C2 = float(2 * np.pi - 6.28125)
            for slot, off in ((1, 0.0), (0, 0.5 * np.pi)):
                ta = ntmp(); tb = ntmp()
                yv = mlaf[0:64, slot, :]
                P.add("dve", lambda e, yv=yv, off=off: e.tensor_scalar(out=yv, in0=mlaf[0:64, 3, :], scalar1=float(off), scalar2=None, op0=ALU.add),
                      r=[mlafB[3]], w=[mlafB[slot]])
                P.add("dve", lambda e, yv=yv, ta=ta: e.tensor_scalar(out=tmp[0:64, ta, 0:T], in0=yv, scalar1=float(1.0 / (2 * np.pi)), scalar2=None,
                                                                     op0=ALU.mult), r=[mlafB[slot]], w=[tmpB[ta]])
                P.add("dve", lambda e, ta=ta: e.tensor_copy(out=kint[:, :], in_=tmp[0:64, ta, 0:T]), r=[tmpB[ta]], w=[kintB])
                P.add("dve", lambda e, ta=ta: e.tensor_copy(out=tmp[0:64, ta, 0:T], in_=kint[:, :]), r=[kintB], w=[tmpB[ta]])
                for cc in (C1, C2):
                    P.add("dve", lambda e, yv=yv, ta=ta, cc=cc: e.scalar_tensor_tensor(out=yv, in0=tmp[0:64, ta, 0:T], scalar=float(-cc), in1=yv,
                                                                                      op0=ALU.mult, op1=ALU.add),
                          r=[tmpB[ta], mlafB[slot]], w=[mlafB[slot]])
                for thr, cmp_, corr in ((np.pi, ALU.is_gt, -2 * np.pi), (-np.pi, ALU.is_lt, 2 * np.pi)):
                    P.add("dve", lambda e, yv=yv, tb=tb, thr=thr, cmp_=cmp_, corr=corr: e.tensor_scalar(
                        out=tmp[0:64, tb, 0:T], in0=yv, scalar1=float(thr), scalar2=float(corr), op0=cmp_, op1=ALU.mult),
                        r=[mlafB[slot]], w=[tmpB[tb]])
                    P.add("dve", lambda e, yv=yv, tb=tb: e.tensor_tensor(out=yv, in0=yv, in1=tmp[0:64, tb, 0:T], op=ALU.add),
                          r=[mlafB[slot], tmpB[tb]], w=[mlafB[slot]])
                P.add("act", lambda e, yv=yv: e.activation(out=yv, in_=yv, func=AF.Sin), r=[mlafB[slot]], w=[mlafB[slot]])
            if cfg.mla_cut == 4:
                return
            P.add("act", lambda e: e.activation(out=sqb[0:64, 3, :], in_=ps[0:64, ppe, 0:T], func=AF.Square), r=[psB[ppe]], w=[sqbB[3]])
            sqk = nbs()
            P.add("act", lambda e: e.activation(out=a2[0:64, sqk, :], in_=sqb[0:64, 3, :], func=AF.Copy), r=[sqbB[3]], w=[a2B[sqk]])
            tk = ntmp()
            P.add("dve", lambda e: e.tensor_scalar(out=tmp[0:64, tk, 0:T], in0=ps[0:64, ppe, 0:T], scalar1=gk_r, scalar2=None, op0=ALU.mult),
                  r=[psB[ppe], glB], w=[tmpB[tk]])
            if cfg.mla_cut == 5:
                return
            rope(tk, mlaf[0:64, 2, :], None)
            if cfg.mla_cut == 6:
                return
            P.add("act", lambda e: e.activation(out=a2[0:64, 23, :], in_=a2[0:64, sqk, :], func=AF.Copy), r=[a2B[sqk]], w=[a2B[23]])
            kvin = lambda kc: a2[:, 8 + kc, :]
            kvB_ = a2B[8:8 + KVC]
            for h in range(MH if cfg.mla_stop >= 2 else 0):
                pk = lin_chunk(mla_w_ukv, KVC, h * 256, kvin, kvB_)
                P.add("act", lambda e, pk=pk: e.activation(out=sqb[:, 0, :], in_=ps[:, pk, 0:T], func=AF.Square), r=[psB[pk]], w=[sqbB[0]])
                pss = nps()
                P.add("pe", lambda e, pss=pss: e.matmul(ps[:, pss, 0:T], onesb, sqb[:, 0, :], start=True, stop=False),
                      r=[sqbB[0], cstB], w=[psB[pss]])
                P.add("pe", lambda e, pss=pss: e.matmul(ps[:, pss, 0:T], onesb[0:64, :], a2[0:64, 23, :], start=False, stop=True),
                      r=[a2B[23], cstB], w=[psB[pss]])
                rk = ntmp()
                P.add("act", lambda e, rk=rk, pss=pss: e.activation(out=tmp[:, rk, 0:T], in_=ps[:, pss, 0:T], func=AF.Sqrt, bias=epsc,
                                                                    scale=1.0 / 192), r=[psB[pss], cstB], w=[tmpB[rk]])
                P.add("dve", lambda e, rk=rk: e.reciprocal(out=tmp[:, rk, 0:T], in_=tmp[:, rk, 0:T]), r=[tmpB[rk]], w=[tmpB[rk]])
                b1 = nbs(); b2 = nbs()
                P.add("dve", lambda e, pk=pk, rk=rk, b1=b1: e.scalar_tensor_tensor(out=a2[:, b1, :], in0=ps[:, pk, 0:T], scalar=gk_n,
                                                                                   in1=tmp[:, rk, 0:T], op0=ALU.mult, op1=ALU.mult),
                      r=[psB[pk], glB, tmpB[rk]], w=[a2B[b1]])
                P.add("dve", lambda e, rk=rk, b2=b2: e.tensor_tensor(out=a2[0:64, b2, :], in0=mlaf[0:64, 2, :], in1=tmp[0:64, rk, 0:T],
                                                                     op=ALU.mult), r=[mlafB[2], tmpB[rk]], w=[a2B[b2]])
                P.add("sp", lambda e, h=h, b1=b1, t=t: e.dma_start(out=kT_scr[h, 0:128, t * T:(t + 1) * T], in_=a2[:, b1, :]),
                      r=[a2B[b1]], w=[dB(("kT", h), t)], dma=True)
                P.add("sp", lambda e, h=h, b2=b2, t=t: e.dma_start(out=kT_scr[h, 128:192, t * T:(t + 1) * T], in_=a2[0:64, b2, :]),
                      r=[a2B[b2]], w=[dB(("kT", h), t)], dma=True)
                wi = load_w(mla_w_ukv, KVC, h * 256 + 128)
                pv = nps()
                for kb in range(4):
                    for kc in range(KVC):
                        P.add("pe", lambda e, kb=kb, kc=kc, wi=wi, pv=pv: e.matmul(
                            ps[:, pv, kb * 128:(kb + 1) * 128], a2[:, 8 + kc, kb * 128:(kb + 1) * 128], wsl[:, wi, kc, 0:128],
                            start=(kc == 0), stop=(kc == KVC - 1)), r=[wB[wi], a2B[8 + kc]], w=[psB[pv]])
                b3 = nbs()
                P.add("act", lambda e, pv=pv, b3=b3: e.activation(out=a2[:, b3, :], in_=ps[:, pv, 0:T], func=AF.Copy), r=[psB[pv]], w=[a2B[b3]])
                P.add("sp", lambda e, h=h, b3=b3, t=t: e.dma_start(
                    out=V_scr[h, t * T:(t + 1) * T, :].rearrange("(kb p) v -> p kb v", p=128),
                    in_=a2[:, b3, :].rearrange("p (kb v) -> p kb v", v=128)), r=[a2B[b3]], w=[dB(("V", h), t)], dma=True)
            qin = lambda kc: a2[:, kc, :]
            nkb = 4 * (t + 1)
            kvld = a2B[24:36]
            for h in range(MH if cfg.mla_stop >= 3 else 0):
                nt_ = t + 1
                P.add("sp", lambda e, h=h, nt_=nt_: e.dma_start(out=a2[:, 24:24 + nt_, :],
                                                                in_=kT_scr[h, 0:128, 0:nt_ * T].rearrange("p (a b) -> p a b", b=T)),
                      r=[dB(("kT", h), tt) for tt in range(nt_)], w=kvld, dma=True)
                P.add("sp", lambda e, h=h, nt_=nt_: e.dma_start(out=a2[0:64, 28:28 + nt_, :],
                                                                in_=kT_scr[h, 128:192, 0:nt_ * T].rearrange("p (a b) -> p a b", b=T)),
                      r=[dB(("kT", h), tt) for tt in range(nt_)], w=kvld, dma=True)
                P.add("sp", lambda e, h=h, nt_=nt_: e.dma_start(
                    out=a2[:, 32:32 + nt_, :].rearrange("p a (kb v) -> p a kb v", v=128),
                    in_=V_scr[h, 0:nt_ * T, :].rearrange("(a kb p) v -> p a kb v", kb=4, p=128)),
                    r=[dB(("V", h), tt) for tt in range(nt_)], w=kvld, dma=True)
                pqn = lin_chunk(mla_w_uq, QRC, h * 192, qin, a2B)
                pqr = lin_chunk(mla_w_uq, QRC, h * 192 + 128, qin, a2B, ncol=64)
                P.add("act", lambda e, pqn=pqn: e.activation(out=sqb[:, 0, :], in_=ps[:, pqn, 0:T], func=AF.Square), r=[psB[pqn]], w=[sqbB[0]])
                P.add("act", lambda e, pqr=pqr: e.activation(out=sqb[0:64, 1, :], in_=ps[0:64, pqr, 0:T], func=AF.Square), r=[psB[pqr]], w=[sqbB[1]])
                pss = nps()
                P.add("pe", lambda e, pss=pss: e.matmul(ps[:, pss, 0:T], onesb, sqb[:, 0, :], start=True, stop=False),
                      r=[sqbB[0], cstB], w=[psB[pss]])
                P.add("pe", lambda e, pss=pss: e.matmul(ps[:, pss, 0:T], onesb[0:64, :], sqb[0:64, 1, :], start=False, stop=True),
                      r=[sqbB[1], cstB], w=[psB[pss]])
                rqh = ntmp()
                P.add("act", lambda e, rqh=rqh, pss=pss: e.activation(out=tmp[:, rqh, 0:T], in_=ps[:, pss, 0:T], func=AF.Sqrt, bias=epsc,
                                                                      scale=1.0 / 192), r=[psB[pss], cstB], w=[tmpB[rqh]])
                P.add("dve", lambda e, rqh=rqh: e.reciprocal(out=tmp[:, rqh, 0:T], in_=tmp[:, rqh, 0:T]), r=[tmpB[rqh]], w=[tmpB[rqh]])
                P.add("dve", lambda e, pqn=pqn, rqh=rqh: e.scalar_tensor_tensor(out=a2[:, 12, :], in0=ps[:, pqn, 0:T], scalar=gq_n,
                                                                                in1=tmp[:, rqh, 0:T], op0=ALU.mult, op1=ALU.mult),
                      r=[psB[pqn], glB, tmpB[rqh]], w=[a2B[12]])
                tq = ntmp()
                P.add("dve", lambda e, tq=tq, pqr=pqr: e.tensor_scalar(out=tmp[0:64, tq, 0:T], in0=ps[0:64, pqr, 0:T], scalar1=gq_r, scalar2=None,
                                                                       op0=ALU.mult), r=[psB[pqr], glB], w=[tmpB[tq]])
                rope(tq, None, True)
                P.add("dve", lambda e, tq=tq, rqh=rqh: e.tensor_tensor(out=a2[0:64, 13, :], in0=tmp[0:64, tq, 0:T], in1=tmp[0:64, rqh, 0:T],
                                                                       op=ALU.mult), r=[tmpB[tq], tmpB[rqh]], w=[a2B[13]])
                pO = 6; pD = 7

                def smm(kb):
                    q0 = max(0, (kb - 4 * t) * 128)
                    pS = nps()
                    ch, co = kb // 4, (kb % 4) * 128
                    P.add("pe", lambda e: e.matmul(ps[:, pS, q0:T], a2[:, 24 + ch, co:co + 128], a2[:, 12, q0:T], start=True, stop=False),
                          r=kvld + [a2B[12]], w=[psB[pS]])
                    P.add("pe", lambda e: e.matmul(ps[:, pS, q0:T], a2[0:64, 28 + ch, co:co + 128], a2[0:64, 13, q0:T], start=False, stop=True),
                          r=kvld + [a2B[13]], w=[psB[pS]])
                    return pS, q0

                if t > 0:
                    order = [(0, 0, 128)] + [(kb, 0, 128) for kb in range(4 * t, 4 * t + 4)] + [(kb, 0, 128) for kb in range(1, 4 * t)]
                else:
                    order = [(0, 0, 64), (1, 0, 128), (2, 0, 128), (3, 0, 128), (0, 64, 128)]
                sdone = {}
                nxt_s = smm(order[0][0])
                sdone[order[0][0]] = None
                for oi_, (kb, p0, p1) in enumerate(order):
                    first = (oi_ == 0); last = (oi_ == len(order) - 1)
                    if kb not in sdone or sdone[kb] is None:
                        pS, q0 = nxt_s
                        pt = nbs()
                        P.add("act", lambda e, pS=pS, q0=q0, pt=pt: e.activation(out=a2[:, pt, q0:T], in_=ps[:, pS, q0:T], func=AF.Exp, scale=SC),
                              r=[psB[pS]], w=[a2B[pt]])
                        if kb >= 4 * t:
                            P.add("dve", lambda e, q0=q0, pt=pt: e.tensor_tensor(out=a2[:, pt, q0:q0 + 128], in0=a2[:, pt, q0:q0 + 128], in1=maskd,
                                                                                 op=ALU.mult), r=[a2B[pt], cstB], w=[a2B[pt]])
                        sdone[kb] = (pt, q0)
                        for kb2, _, _ in order[oi_ + 1:]:
                            if kb2 not in sdone:
                                nxt_s = smm(kb2); sdone[kb2] = None
                                break
                    pt, q0 = sdone[kb]
                    ch, co = kb // 4, (kb % 4) * 128
                    P.add("pe", lambda e, q0=q0, pt=pt, ch=ch, co=co, p0=p0, p1=p1, first=first, last=last: e.matmul(
                        ps[:, pO, q0:T], a2[p0:p1, 32 + ch, co:co + 128], a2[p0:p1, pt, q0:T], start=first, stop=last),
                        r=kvld + [a2B[pt]], w=[psB[pO]])
                    P.add("pe", lambda e, q0=q0, pt=pt, p0=p0, p1=p1, first=first, last=last: e.matmul(
                        ps[:, pD, q0:T], onesb[p0:p1, :], a2[p0:p1, pt, q0:T], start=first, stop=last),
                        r=[cstB, a2B[pt]], w=[psB[pD]])
                rd = ntmp()
                P.add("dve", lambda e, rd=rd: e.reciprocal(out=tmp[:, rd, 0:T], in_=ps[:, pD, 0:T]), r=[psB[pD]], w=[tmpB[rd]])
                P.add("dve", lambda e, rd=rd, h=h: e.tensor_tensor(out=hT[:, h, :], in0=ps[:, pO, 0:T], in1=tmp[:, rd, 0:T], op=ALU.mult),
                      r=[psB[pO], tmpB[rd]], w=[hB[h]])
            out_proj(mla_w_o, MH, lambda kc: hT[:, kc, :], hB, xsrc, xname, xdst, dname, l, 1, t)

        for t in range(NT):
            mla_tile(t)


    def gdn(l, xsrc, xname, xdst, dname):
        GK, GV = cfg.GK, cfg.GV
        REP = GV // GK
        GKD, GVD, GQKV = cfg.GKD, cfg.GVD, cfg.GQKV
        NCH = GQKV // 128
        S_scr = dscr("S_scr", [GV, 128, 128])
        P.add("sp", lambda e: e.dma_start(out=vec[:, 0:4 * NCH], in_=gdn_conv_w), w=[vecB], dma=True)
        hvb = carve(0, 4, F32, 64); hvbB = Buf("hvb")
        gog = carve(4, 1)
        P.add("sp", lambda e: e.dma_start(out=hvb[0:GV, 0:2], in_=gdn_hv[0:GV, :]), w=[hvbB], dma=True)
        P.add("sp", lambda e: e.dma_start(out=gog, in_=gdn_o_g), w=[hvbB], dma=True)
        P.add("act", lambda e: e.activation(out=hvb[0:GV, 2:3], in_=hvb[0:GV, 0:1], func=AF.Exp), r=[hvbB], w=[hvbB])
        P.add("dve", lambda e: e.tensor_scalar(out=hvb[0:GV, 2:3], in0=hvb[0:GV, 2:3], scalar1=-1.0, scalar2=None, op0=ALU.mult),
              r=[hvbB], w=[hvbB])
        Sst = carve(16, REP * 128).rearrange("p (a b) -> p a b", a=REP); SstB = [Buf(f"S{j}") for j in range(REP)]
        qkvb = carve(16 + REP * 128, (2 + REP) * T // 2, BF16).rearrange("p (a b) -> p a b", b=T)
        qkvB = [Buf(f"qkv{i}") for i in range(2 + REP)]
        o0 = 16 + REP * 128 + (2 + REP) * T // 2
        oacc = carve(o0, REP * T).rearrange("p (a b) -> p a b", a=REP); oaccB = [Buf(f"oacc{j}") for j in range(REP)]
        o1 = o0 + REP * T
        NCK = T // 64
        tsc = carve(o1, NCK * 5 * GV, F32, 64).rearrange("p (a b) -> p a b", a=NCK); tscB = [Buf(f"tsc{c}") for c in range(NCK)]
        o2 = o1 + NCK * 5 * GV
        KKs = carve(o2, 64, F32, 64); QKTs = carve(o2 + 64, 64, F32, 64); ktok = carve(o2 + 128, 128, F32, 64)
        shB = [Buf("KKs"), Buf("QKTs"), Buf("ktok")]
        o3 = o2 + 256
        HW_ = 1408

        class HS:
            pass
        hs = []
        for j in range(REP):
            b0 = o3 + j * HW_
            h_ = HS()
            h_.R = [carve(b0, 64, F32, 64), carve(b0 + 64, 64, F32, 64)]
            h_.RT = [carve(b0 + 128, 64, F32, 64), carve(b0 + 192, 64, F32, 64)]
            h_.Y = carve(b0 + 256, 64, F32, 64)
            h_.MBU = carve(b0 + 320, 64, F32, 64)
            h_.DT = carve(b0 + 384, 64, F32, 64)
            h_.AT = carve(b0 + 448, 64, F32, 64)
            h_.vb = carve(b0 + 512, 128, F32, 64); h_.kbg = carve(b0 + 640, 128, F32, 64)
            h_.kd = carve(b0 + 768, 128, F32, 64); h_.vnew = carve(b0 + 896, 128, F32, 64)
            h_.wTn = carve(b0 + 1024, 64); h_.qg = carve(b0 + 1088, 64)
            h_.vtok = carve(b0 + 1152, 128, F32, 64)
            h_.egl = carve(b0 + 1280, 1)
            h_.B = {n: Buf(f"{n}{j}") for n in ("R0", "R1", "RT0", "RT1", "Y", "MBU", "DT", "AT", "vb", "kbg", "kd", "vnew", "wTn", "qg",
                                                 "vtok", "egl")}
            hs.append(h_)
        assert o3 + REP * HW_ <= MXW
        fs = a2[0:GV, 32:44, :].rearrange("p a b -> p (a b)").bitcast(F32).rearrange("p (a b) -> p a b", b=T)
        fsB = a2B[32:44]
        identf = cst[:, 0:128]
        strictU = cst[0:64, 384:448]; incU = cst[0:64, 448:512]
        onesf = cst[:, 128:256]
        one_c = cst[:, 707:708]

        def mm(out_ap, lhsT, rhs, r, wbuf, start=True, stop=True):
            P.add("pe", lambda e: e.matmul(out_ap, lhsT, rhs, start=start, stop=stop), r=r, w=[wbuf])

        def gdn_tile(t):
            adaln(xsrc, xname, l, 1, t)
            hin = lambda kc: hT[:, kc, :]
            pb_ = lin_chunk(gdn_w_in, DC, GQKV + GVD, hin, hB, ncol=GV)
            P.add("act", lambda e: e.activation(out=fs[:, 0, :], in_=ps[0:GV, pb_, 0:T], func=AF.Sigmoid), r=[psB[pb_]], w=fsB)
            pa_ = lin_chunk(gdn_w_in, DC, GQKV + GVD + GV, hin, hB, ncol=GV)
            P.add("act", lambda e: e.activation(out=fs[:, 5, :], in_=ps[0:GV, pa_, 0:T], func=AF.Identity, bias=hvb[0:GV, 1:2]),
                  r=[psB[pa_], hvbB], w=fsB)
            ta = ntmp()
            P.add("act", lambda e: e.activation(out=tmp[0:GV, ta, 0:T], in_=fs[:, 5, :], func=AF.Abs), r=fsB, w=[tmpB[ta]])
            P.add("act", lambda e: e.activation(out=tmp[0:GV, ta, 0:T], in_=tmp[0:GV, ta, 0:T], func=AF.Exp, scale=-1.0), r=[tmpB[ta]], w=[tmpB[ta]])
            P.add("act", lambda e: e.activation(out=tmp[0:GV, ta, 0:T], in_=tmp[0:GV, ta, 0:T], func=AF.Ln, bias=one_c[0:GV, :]),
                  r=[tmpB[ta], cstB], w=[tmpB[ta]])
            P.add("dve", lambda e: e.tensor_scalar(out=fs[:, 5, :], in0=fs[:, 5, :], scalar1=0.0, scalar2=None, op0=ALU.max), r=fsB, w=fsB)
            P.add("dve", lambda e: e.tensor_tensor(out=fs[:, 5, :], in0=fs[:, 5, :], in1=tmp[0:GV, ta, 0:T], op=ALU.add),
                  r=fsB + [tmpB[ta]], w=fsB)
            P.add("dve", lambda e: e.tensor_scalar(out=fs[:, 5, :], in0=fs[:, 5, :], scalar1=hvb[0:GV, 2:3], scalar2=None, op0=ALU.mult),
                  r=fsB + [hvbB], w=fsB)
            for c in range(NCK):
                cs = slice(c * 64, (c + 1) * 64)
                P.add("dve", lambda e, cs=cs: e.tensor_tensor_scan(out=fs[:, 1, cs], data0=onesf[0:GV, 0:64], data1=fs[:, 5, cs], initial=0.0,
                                                                   op0=ALU.mult, op1=ALU.add), r=fsB + [cstB], w=fsB)
            P.add("act", lambda e: e.activation(out=fs[:, 2, :], in_=fs[:, 1, :], func=AF.Exp), r=fsB, w=fsB)
            P.add("dve", lambda e: e.tensor_tensor(out=fs[:, 3, :], in0=fs[:, 0, :], in1=fs[:, 2, :], op=ALU.mult), r=fsB, w=fsB)
            for c in range(NCK):
                cs = slice(c * 64, (c + 1) * 64)
                P.add("act", lambda e, cs=cs, c=c: e.activation(out=fs[:, 4, cs], in_=fs[:, 1, cs], func=AF.Exp, scale=-1.0,
                                                                bias=fs[:, 1, c * 64 + 63:c * 64 + 64]), r=fsB, w=fsB)
            for c in range(NCK):
                cs = slice(c * 64, (c + 1) * 64)
                pt_ = nps()
                for qi in range(5):
                    mm(ps[0:64, pt_, qi * GV:(qi + 1) * GV], fs[:, qi, cs], identf[0:GV, 0:GV], fsB + [cstB], psB[pt_])
                P.add("act", lambda e, c=c, pt_=pt_: e.activation(out=tsc[:, c, :], in_=ps[0:64, pt_, 0:5 * GV], func=AF.Copy),
                      r=[psB[pt_]], w=[tscB[c]])
            for g in range(GK):
                gdn_group(t, g, hin)
            out_proj(gdn_w_out, GV, lambda kc: a2[:, kc, :], a2B, xsrc, xname, xdst, dname, l, 1, t)

        def proj_conv(t, col, cidx, hin):
            pi = lin_chunk(gdn_w_in, DC, col, hin, hB)
            zt = ntmp()
            P.add("act", lambda e: e.activation(out=tmp[:, zt, 3:3 + T], in_=ps[:, pi, 0:T], func=AF.Copy), r=[psB[pi]], w=[tmpB[zt]])
            oi = conv_chunk(zt, hal, halB, cidx, 4, lambda j: vec[:, j * NCH + cidx:j * NCH + cidx + 1], None, t)
            P.add("act", lambda e: e.activation(out=tmp[:, oi, 0:T], in_=tmp[:, oi, 0:T], func=AF.Silu), r=[tmpB[oi]], w=[tmpB[oi]])
            return oi

        def l2n(oi, dst, dstB, mult):
            P.add("act", lambda e: e.activation(out=sqb[:, 0, :], in_=tmp[:, oi, 0:T], func=AF.Square), r=[tmpB[oi]], w=[sqbB[0]])
            pss = nps()
            mm(ps[:, pss, 0:T], onesb, sqb[:, 0, :], [sqbB[0], cstB], psB[pss])
            rr = ntmp()
            P.add("act", lambda e: e.activation(out=tmp[:, rr, 0:T], in_=ps[:, pss, 0:T], func=AF.Sqrt, bias=epsc), r=[psB[pss], cstB], w=[tmpB[rr]])
            P.add("dve", lambda e: e.reciprocal(out=tmp[:, rr, 0:T], in_=tmp[:, rr, 0:T]), r=[tmpB[rr]], w=[tmpB[rr]])
            P.add("dve", lambda e: e.scalar_tensor_tensor(out=dst, in0=tmp[:, oi, 0:T], scalar=float(mult), in1=tmp[:, rr, 0:T],
                                                          op0=ALU.mult, op1=ALU.mult), r=[tmpB[oi], tmpB[rr]], w=[dstB])

        def gdn_group(t, g, hin):
            heads = [REP * g + j for j in range(REP)]
            for j, h in enumerate(heads):
                if t == 0:
                    P.add("dve", lambda e, j=j: e.memset(Sst[:, j, :], 0.0), w=[SstB[j]])
                else:
                    P.add("sp", lambda e, j=j, h=h: e.dma_start(out=Sst[:, j, :], in_=S_scr[h]), r=[dB(("S", h), 0)], w=[SstB[j]], dma=True)
            oq = proj_conv(t, g * 128, g, hin)
            l2n(oq, qkvb[:, 0, :], qkvB[0], 128.0 ** -0.5)
            ok = proj_conv(t, GKD + g * 128, GK + g, hin)
            l2n(ok, qkvb[:, 1, :], qkvB[1], 1.0)
            for j, h in enumerate(heads):
                ov = proj_conv(t, 2 * GKD + h * 128, 2 * GK + h, hin)
                P.add("act", lambda e, j=j, ov=ov: e.activation(out=qkvb[:, 2 + j, :], in_=tmp[:, ov, 0:T], func=AF.Copy), r=[tmpB[ov]], w=[qkvB[2 + j]])
            for c in range(NCK):
                gdn_chunk(t, g, heads, c)
            for j, h in enumerate(heads):
                P.add("act", lambda e, j=j: e.activation(out=sqb[:, 1, :], in_=oacc[:, j, :], func=AF.Square), r=[oaccB[j]], w=[sqbB[1]])
                pss = nps()
                mm(ps[:, pss, 0:T], onesb, sqb[:, 1, :], [sqbB[1], cstB], psB[pss])
                rr = ntmp()
                P.add("act", lambda e, rr=rr, pss=pss: e.activation(out=tmp[:, rr, 0:T], in_=ps[:, pss, 0:T], func=AF.Sqrt, bias=epsc,
                                                                    scale=1.0 / 128), r=[psB[pss], cstB], w=[tmpB[rr]])
                P.add("dve", lambda e, rr=rr: e.reciprocal(out=tmp[:, rr, 0:T], in_=tmp[:, rr, 0:T]), r=[tmpB[rr]], w=[tmpB[rr]])
                P.add("dve", lambda e, rr=rr, j=j: e.tensor_tensor(out=tmp[:, rr, 0:T], in0=oacc[:, j, :], in1=tmp[:, rr, 0:T], op=ALU.mult),
                      r=[oaccB[j], tmpB[rr]], w=[tmpB[rr]])
                pz = lin_chunk(gdn_w_in, DC, GQKV + h * 128, hin, hB)
                sz = ntmp()
                P.add("act", lambda e, sz=sz, pz=pz: e.activation(out=tmp[:, sz, 0:T], in_=ps[:, pz, 0:T], func=AF.Silu), r=[psB[pz]], w=[tmpB[sz]])
                P.add("dve", lambda e, rr=rr, sz=sz, h=h: e.scalar_tensor_tensor(out=a2[:, h, :], in0=tmp[:, rr, 0:T], scalar=gog, in1=tmp[:, sz, 0:T],
                                                                                op0=ALU.mult, op1=ALU.mult),
                      r=[tmpB[rr], tmpB[sz], hvbB], w=[a2B[h]])
                P.add("sp", lambda e, j=j, h=h: e.dma_start(out=S_scr[h], in_=Sst[:, j, :]), r=[SstB[j]], w=[dB(("S", h), 0)], dma=True)

        def gdn_chunk(t, g, heads, c):
            cs = slice(c * 64, (c + 1) * 64)
            kq = qkvb[:, 1, cs]; qq = qkvb[:, 0, cs]
            p1 = nps()
            mm(ps[0:64, p1, 0:128], kq, identb, [qkvB[1], cstB], psB[p1])
            P.add("act", lambda e: e.activation(out=ktok, in_=ps[0:64, p1, 0:128], func=AF.Copy), r=[psB[p1]], w=[shB[2]])
            p2 = nps()
            mm(ps[0:64, p2, 0:64], kq, kq, [qkvB[1]], psB[p2])
            P.add("act", lambda e: e.activation(out=KKs, in_=ps[0:64, p2, 0:64], func=AF.Copy), r=[psB[p2]], w=[shB[0]])
            p3 = nps()
            mm(ps[0:64, p3, 0:64], kq, qq, [qkvB[0], qkvB[1]], psB[p3])
            P.add("act", lambda e: e.activation(out=QKTs, in_=ps[0:64, p3, 0:64], func=AF.Copy), r=[psB[p3]], w=[shB[1]])
            sc = lambda qi, h: tsc[:, c, qi * GV + h:qi * GV + h + 1]
            for j, h in enumerate(heads):
                H = hs[j]; B = H.B
                p4 = nps()
                mm(ps[0:64, p4, 0:128], qkvb[:, 2 + j, cs], identb, [qkvB[2 + j], cstB], psB[p4])
                P.add("act", lambda e, H=H, p4=p4: e.activation(out=H.vtok, in_=ps[0:64, p4, 0:128], func=AF.Copy), r=[psB[p4]], w=[B["vtok"]])
                p5 = nps()
                esel = identf[0:GV, h:h + 1].to_broadcast([GV, 64])
                mm(ps[0:64, p5, 0:64], esel, fs[:, 1, cs], fsB + [cstB], psB[p5])
                mm(ps[0:64, p5, 64:128], esel, fs[:, 0, cs], fsB + [cstB], psB[p5])
                P.add("dve", lambda e, H=H, p5=p5, h=h: e.tensor_scalar(out=H.DT, in0=ps[0:64, p5, 0:64], scalar1=sc(1, h), scalar2=0.0,
                                                                        op0=ALU.subtract, op1=ALU.min), r=[psB[p5], tscB[c]], w=[B["DT"]])
                P.add("act", lambda e, H=H: e.activation(out=H.DT, in_=H.DT, func=AF.Exp), r=[B["DT"]], w=[B["DT"]])
                P.add("dve", lambda e, H=H, p5=p5: e.scalar_tensor_tensor(out=H.MBU, in0=ps[0:64, p5, 64:128], scalar=-1.0, in1=strictU,
                                                                          op0=ALU.mult, op1=ALU.mult), r=[psB[p5], cstB], w=[B["MBU"]])
                P.add("dve", lambda e, H=H: e.tensor_tensor(out=H.RT[0], in0=KKs, in1=H.DT, op=ALU.mult), r=[shB[0], B["DT"]], w=[B["RT0"]])
                P.add("dve", lambda e, H=H: e.tensor_tensor(out=H.RT[0], in0=H.RT[0], in1=H.MBU, op=ALU.mult), r=[B["RT0"], B["MBU"]], w=[B["RT0"]])
                P.add("dve", lambda e, H=H: e.tensor_tensor(out=H.AT, in0=QKTs, in1=H.DT, op=ALU.mult), r=[shB[1], B["DT"]], w=[B["AT"]])
                P.add("dve", lambda e, H=H: e.tensor_tensor(out=H.AT, in0=H.AT, in1=incU, op=ALU.mult), r=[B["AT"], cstB], w=[B["AT"]])
                p6 = nps()
                mm(ps[0:64, p6, 0:64], H.RT[0], identf[0:64, 0:64], [B["RT0"], cstB], psB[p6])
                P.add("act", lambda e, H=H, p6=p6: e.activation(out=H.R[0], in_=ps[0:64, p6, 0:64], func=AF.Copy), r=[psB[p6]], w=[B["R0"]])
                P.add("dve", lambda e, H=H: e.tensor_tensor(out=H.Y, in0=H.RT[0], in1=identf[0:64, 0:64], op=ALU.add), r=[B["RT0"], cstB], w=[B["Y"]])
                P.add("dve", lambda e, H=H, h=h: e.tensor_scalar(out=H.vb, in0=H.vtok, scalar1=sc(0, h), scalar2=None, op0=ALU.mult),
                      r=[B["vtok"], tscB[c]], w=[B["vb"]])
                P.add("dve", lambda e, H=H, h=h: e.tensor_scalar(out=H.kbg, in0=ktok, scalar1=sc(3, h), scalar2=None, op0=ALU.mult),
                      r=[shB[2], tscB[c]], w=[B["kbg"]])
                P.add("dve", lambda e, H=H, h=h: e.tensor_scalar(out=H.kd, in0=ktok, scalar1=sc(4, h), scalar2=None, op0=ALU.mult),
                      r=[shB[2], tscB[c]], w=[B["kd"]])
            cur = 0
            for k in range(1, 6):
                nx = 1 - cur
                for j, h in enumerate(heads):
                    H = hs[j]; B = H.B
                    pr_ = nps()
                    mm(ps[0:64, pr_, 0:64], H.RT[cur], H.R[cur], [B[f"RT{cur}"], B[f"R{cur}"]], psB[pr_])
                    P.add("act", lambda e, H=H, pr_=pr_, nx=nx: e.activation(out=H.R[nx], in_=ps[0:64, pr_, 0:64], func=AF.Copy),
                          r=[psB[pr_]], w=[B[f"R{nx}"]])
                    if k < 5:
                        prt = nps()
                        mm(ps[0:64, prt, 0:64], H.R[cur], H.RT[cur], [B[f"RT{cur}"], B[f"R{cur}"]], psB[prt])
                        P.add("act", lambda e, H=H, prt=prt, nx=nx: e.activation(out=H.RT[nx], in_=ps[0:64, prt, 0:64], func=AF.Copy),
                              r=[psB[prt]], w=[B[f"RT{nx}"]])
                    py = nps()
                    mm(ps[0:64, py, 0:64], H.R[nx], H.Y, [B[f"R{nx}"], B["Y"]], psB[py])
                    P.add("dve", lambda e, H=H, py=py: e.tensor_tensor(out=H.Y, in0=ps[0:64, py, 0:64], in1=H.Y, op=ALU.add),
                          r=[psB[py], B["Y"]], w=[B["Y"]])
                cur = nx
            for j, h in enumerate(heads):
                H = hs[j]; B = H.B
                pw = nps()
                mm(ps[:, pw, 0:64], H.kbg, H.Y, [B["kbg"], B["Y"]], psB[pw])
                P.add("act", lambda e, H=H, pw=pw: e.activation(out=H.wTn, in_=ps[:, pw, 0:64], func=AF.Copy, scale=-1.0), r=[psB[pw]], w=[B["wTn"]])
                pv_ = nps()
                mm(ps[0:64, pv_, 0:128], H.Y, H.vb, [B["Y"], B["vb"]], psB[pv_], start=True, stop=False)
                mm(ps[0:64, pv_, 0:128], H.wTn, Sst[:, j, :], [B["wTn"], SstB[j]], psB[pv_], start=False, stop=True)
                P.add("act", lambda e, H=H, pv_=pv_: e.activation(out=H.vnew, in_=ps[0:64, pv_, 0:128], func=AF.Copy), r=[psB[pv_]], w=[B["vnew"]])
                pe_ = nps()
                esel128 = identf[0:GV, h:h + 1].to_broadcast([GV, 128])
                mm(ps[:, pe_, 0:64], esel128, fs[:, 2, cs], fsB + [cstB], psB[pe_])
                P.add("dve", lambda e, H=H, pe_=pe_: e.tensor_tensor(out=H.qg, in0=qq, in1=ps[:, pe_, 0:64], op=ALU.mult),
                      r=[qkvB[0], psB[pe_]], w=[B["qg"]])
                P.add("dve", lambda e, H=H, pe_=pe_: e.tensor_copy(out=H.egl, in_=ps[:, pe_, 63:64]), r=[psB[pe_]], w=[B["egl"]])
                po = nps()
                mm(ps[:, po, 0:64], Sst[:, j, :], H.qg, [SstB[j], B["qg"]], psB[po], start=True, stop=False)
                mm(ps[:, po, 0:64], H.vnew, H.AT, [B["vnew"], B["AT"]], psB[po], start=False, stop=True)
                P.add("act", lambda e, po=po, j=j: e.activation(out=oacc[:, j, cs], in_=ps[:, po, 0:64], func=AF.Copy), r=[psB[po]], w=[oaccB[j]])
                pS_ = nps()
                mm(ps[:, pS_, 0:128], H.kd, H.vnew, [B["kd"], B["vnew"]], psB[pS_])
                P.add("dve", lambda e, H=H, pS_=pS_, j=j: e.scalar_tensor_tensor(out=Sst[:, j, :], in0=Sst[:, j, :], scalar=H.egl, in1=ps[:, pS_, 0:128],
                                                                                 op0=ALU.mult, op1=ALU.add),
                      r=[SstB[j], B["egl"], psB[pS_]], w=[SstB[j]])

        for t in range(NT):
            gdn_tile(t)

    modulation()
    cur, curname = xT, "xT"
    nsub = 0
    total_sub = len(cfg.stages) * len(cfg.layers)

    def nxt():
        nonlocal nsub
        nsub += 1
        if nsub == total_sub:
            return yT, "yT"
        return xs[nsub % 2], f"xs{nsub % 2}"

    for l in cfg.layers:
        m = l % 4
        for si, stg in enumerate(cfg.stages):
            dst, dname = nxt()
            if stg == 'f':
                which = 0 if si == 0 else 1
                ffn(l, which, 0 if which == 0 else 2, cur, curname, dst, dname)
            elif m == 0:
                sconv(l, cur, curname, dst, dname)
            elif m == 1:
                mla(l, cur, curname, dst, dname)
            elif m == 2:
                rglru(l, cur, curname, dst, dname)
            else:
                gdn(l, cur, curname, dst, dname)
            cur, curname = dst, dname
    if cfg.debug:
        dbg = nc.dram_tensor("dbg", [128, cfg.DEPTH, 9 * DC], F32, kind="ExternalOutput").ap()
        for l_ in cfg.layers:
            P.add("sp", lambda e, l_=l_: e.dma_start(out=dbg[:, l_, :], in_=drv[:, l_, :]), r=[drvB], dma=True)
    P.emit(nc, es)
    es.close()
    return nc


def prep_inputs(cfg, inp, b):
    D, DC = cfg.D, cfg.DC
    f = np.float32

    def pc(v):
        v = np.asarray(v, f)
        return np.ascontiguousarray(v.reshape(-1, 128).T)

    d = {}
    d["xT"] = np.ascontiguousarray(np.asarray(inp["x"][b], f).T)
    d["cvec"] = pc(inp["c"][b])
    d["positions"] = np.ascontiguousarray(np.asarray(inp["positions"][b], np.int32)[None, :])
    d["consts"] = make_consts()
    d["cond_w"] = np.asarray(inp["cond_w"], f)
    d["cond_b"] = pc(inp["cond_b"])
    d["mod_w"] = np.asarray(inp["mod_w"], f)
    d["mod_b"] = np.ascontiguousarray(np.stack([pc(inp["mod_b"][l]) for l in range(cfg.DEPTH)]))
    d["norm_g"] = np.ascontiguousarray(np.stack(
        [np.concatenate([pc(inp["norm_g"][l, s]) for s in range(3)], axis=1) for l in range(cfg.DEPTH)]))
    d["ffn_w_in"] = np.asarray(inp["ffn_w_in"], f)
    d["ffn_w_out"] = np.asarray(inp["ffn_w_out"], f)
    d["sc_w_in"] = np.asarray(inp["sc_w_in"][0], f)
    d["sc_conv_w"] = np.ascontiguousarray(np.concatenate([pc(inp["sc_conv_w"][0, j]) for j in range(3)], axis=1))
    d["sc_w_out"] = np.asarray(inp["sc_w_out"][0], f)
    d["mla_w_in"] = np.asarray(inp["mla_w_in"][0], f)
    d["mla_lat_g"] = np.ascontiguousarray(np.concatenate([pc(inp["mla_q_lat_g"][0]), pc(inp["mla_kv_lat_g"][0])], axis=1))
    d["mla_w_uq"] = np.asarray(inp["mla_w_uq"][0], f)
    d["mla_w_ukv"] = np.asarray(inp["mla_w_ukv"][0], f)
    qg = np.asarray(inp["mla_q_norm_g"][0], f); kg = np.asarray(inp["mla_k_norm_g"][0], f)
    g4 = np.zeros((128, 4), f)
    g4[:, 0] = qg[:128]; g4[:64, 1] = qg[128:]; g4[:, 2] = kg[:128]; g4[:64, 3] = kg[128:]
    d["mla_qk_g"] = g4
    d["mla_w_o"] = np.asarray(inp["mla_w_o"][0], f)
    d["lru_w_in"] = np.asarray(inp["lru_w_in"][0], f)
    d["lru_vec"] = np.ascontiguousarray(np.concatenate(
        [pc(inp["lru_conv_w"][0, j]) for j in range(4)] + [pc(inp["lru_conv_b"][0]), pc(inp["lru_b_a"][0]),
                                                           pc(inp["lru_b_x"][0]), pc(inp["lru_lam"][0])], axis=1))
    d["lru_w_a"] = np.asarray(inp["lru_w_a"][0], f)
    d["lru_w_x"] = np.asarray(inp["lru_w_x"][0], f)
    d["lru_w_out"] = np.asarray(inp["lru_w_out"][0], f)
    d["gdn_w_in"] = np.asarray(inp["gdn_w_in"][0], f)
    d["gdn_conv_w"] = np.ascontiguousarray(np.concatenate([pc(inp["gdn_conv_w"][0, j]) for j in range(4)], axis=1))
    hv = np.zeros((64, 2), f)
    hv[0:cfg.GV, 0] = np.asarray(inp["gdn_a_log"][0], f); hv[0:cfg.GV, 1] = np.asarray(inp["gdn_dt_bias"][0], f)
    d["gdn_hv"] = hv
    d["gdn_o_g"] = np.ascontiguousarray(np.asarray(inp["gdn_o_norm_g"][0], f)[:, None])
    d["gdn_w_out"] = np.asarray(inp["gdn_w_out"][0], f)
    return d


_SHARED = ["consts", "cond_w", "cond_b", "mod_w", "mod_b", "norm_g", "ffn_w_in", "ffn_w_out", "sc_w_in", "sc_conv_w",
           "sc_w_out", "mla_w_in", "mla_lat_g", "mla_w_uq", "mla_w_ukv", "mla_qk_g", "mla_w_o", "lru_w_in", "lru_vec",
           "lru_w_a", "lru_w_x", "lru_w_out", "gdn_w_in", "gdn_conv_w", "gdn_hv", "gdn_o_g", "gdn_w_out"]


def run(cfg, inp, trace=False):
    nc = build(cfg)
    shared = prep_inputs(cfg, inp, 0)
    maps = []
    for b in range(cfg.NCORES):
        d = dict(shared)
        f = np.float32
        d["xT"] = np.ascontiguousarray(np.asarray(inp["x"][b], f).T)
        d["cvec"] = np.ascontiguousarray(np.asarray(inp["c"][b], f).reshape(-1, 128).T)
        d["positions"] = np.ascontiguousarray(np.asarray(inp["positions"][b], np.int32)[None, :])
        maps.append(d)
    res = run_bass_kernel_spmd(nc, maps, core_ids=list(range(cfg.NCORES)), trace=trace)
    out = np.stack([np.ascontiguousarray(r["yT"].T) for r in res.results], axis=0)
    return out, res


def kernel(**inputs):
    cfg = Cfg()
    out, _ = run(cfg, inputs)
    return out.astype(np.float32)
```
